# Optimizing a Trainium2 kernel written in Bass

```python
import math
import jax, jax.numpy as jnp
from jax import lax
import numpy as np

D_MODEL = 2048
BATCH = 16
SEQ = 2048
DEPTH = 2
DEC_BATCH = 16
DEC_SEQ = 16
PAST_LEN = 1024

CHUNK = 64
Q_BLOCK = 128
N_LAYERS_A = (DEPTH + 1) // 2
N_LAYERS_B = DEPTH // 2

A_HEAD_DIM = 64
A_HEADS = D_MODEL // (2 * A_HEAD_DIM)
A_QK = 2 * A_HEADS * A_HEAD_DIM
A_V = A_HEADS * 2 * A_HEAD_DIM
A_IN = 2 * A_QK + 2 * A_V
A_ROT_DIM = A_HEAD_DIM // 4
ROPE_THETA = 500000.0

B_HEADS = 8
B_QK = D_MODEL
B_V = 2 * D_MODEL
B_DK = B_QK // B_HEADS
B_DV = B_V // B_HEADS
B_IN = 2 * B_QK + 2 * B_V
XPOS_BASE = 10000.0

NORM_EPS = 1e-6
SUBLN_EPS = 1e-5

kernel_name = 'chunk_causal_diffattn_retention_hybrid_step'


def rmsnorm(x, g, eps):
    xf = x.astype(jnp.float32)
    y = xf * lax.rsqrt(jnp.mean(xf * xf, axis=-1, keepdims=True) + eps)
    return (y * g.astype(jnp.float32)).astype(x.dtype)


def head_rmsnorm(x, eps):
    xf = x.astype(jnp.float32)
    y = xf * lax.rsqrt(jnp.mean(xf * xf, axis=-1, keepdims=True) + eps)
    return y.astype(x.dtype)


def rotate(x, angles):
    half = angles.shape[-1]
    cos = jnp.cos(angles)[None, :, None, :].astype(x.dtype)
    sin = jnp.sin(angles)[None, :, None, :].astype(x.dtype)
    x1 = x[..., :half]
    x2 = x[..., half:2 * half]
    rest = x[..., 2 * half:]
    return jnp.concatenate([x1 * cos - x2 * sin, x1 * sin + x2 * cos, rest], axis=-1)


def partial_rope_angles(pos):
    inv = ROPE_THETA ** (-jnp.arange(0, A_ROT_DIM, 2, dtype=jnp.float32) / A_ROT_DIM)
    return pos[:, None] * inv[None, :]


def xpos_angles(pos):
    inv = 1.0 / (XPOS_BASE ** jnp.linspace(0.0, 1.0, B_DK // 2, dtype=jnp.float32))
    return pos[:, None] * inv[None, :]


def diff_attend(q, k, v, lam, mask):
    b, lq = q.shape[0], q.shape[1]
    lk = k.shape[1]
    s = jnp.einsum('bqhd,bkhd->bhqk', q, k).astype(jnp.float32)
    if mask is not None:
        s = jnp.where(mask, s, -jnp.inf)
    p = jax.nn.softmax(s, axis=-1).reshape(b, A_HEADS, 2, lq, lk)
    a = p[:, :, 0] - lam * p[:, :, 1]
    return jnp.einsum('bhqk,bkhe->bqhe', a.astype(v.dtype), v)


def diff_layer(xp, xs, ck, cv, norm_g, w_in, lq1, lk1, lq2, lk2, subln_g, w_out, layer_idx):
    lambda_init = 0.8 - 0.6 * math.exp(-0.3 * layer_idx)
    lam = (jnp.exp(jnp.sum(lq1 * lk1).astype(jnp.float32))
           - jnp.exp(jnp.sum(lq2 * lk2).astype(jnp.float32)) + lambda_init)
    scale = A_HEAD_DIM ** -0.5

    def project(x, pos):
        b, l = x.shape[0], x.shape[1]
        z = rmsnorm(x, norm_g, NORM_EPS) @ w_in
        q = z[..., :A_QK].reshape(b, l, 2 * A_HEADS, A_HEAD_DIM)
        k = z[..., A_QK:2 * A_QK].reshape(b, l, 2 * A_HEADS, A_HEAD_DIM)
        v = z[..., 2 * A_QK:2 * A_QK + A_V].reshape(b, l, A_HEADS, 2 * A_HEAD_DIM)
        g = z[..., 2 * A_QK + A_V:]
        ang = partial_rope_angles(pos)
        return rotate(q, ang) * scale, rotate(k, ang), v, g

    def finish(x, o, g):
        b, l = x.shape[0], x.shape[1]
        o = rmsnorm(o, subln_g, SUBLN_EPS) * (1.0 - lambda_init)
        return x + (o.reshape(b, l, A_V) * jax.nn.silu(g)) @ w_out

    s_len = xp.shape[1]
    qp, kp, vp, gp = project(xp, jnp.arange(s_len, dtype=jnp.float32))
    outs = []
    for blk in range(s_len // Q_BLOCK):
        q0 = blk * Q_BLOCK
        kend = q0 + Q_BLOCK
        q_chunk = (q0 + jnp.arange(Q_BLOCK)) // CHUNK
        k_chunk = jnp.arange(kend) // CHUNK
        mask = (k_chunk[None, :] <= q_chunk[:, None])[None, None]
        outs.append(diff_attend(qp[:, q0:kend], kp[:, :kend], vp[:, :kend], lam, mask))
    yp = finish(xp, jnp.concatenate(outs, axis=1), gp)

    past = ck.shape[1]
    qs, ks, vs, gs = project(xs, past + jnp.arange(xs.shape[1], dtype=jnp.float32))
    k_all = jnp.concatenate([ck, ks], axis=1)
    v_all = jnp.concatenate([cv, vs], axis=1)
    ys = finish(xs, diff_attend(qs, k_all, v_all, lam, None), gs)
    return yp, ys, kp, vp, ks, vs


def retention_chunk(s_state, q, k, v, log_gamma):
    l = q.shape[1]
    idx = jnp.arange(l, dtype=jnp.float32)
    lg = log_gamma[:, None]
    intra = jnp.exp(lg[:, :, None] * jnp.abs(idx[:, None] - idx[None, :])).astype(q.dtype)
    scores = jnp.einsum('bqhd,bkhd->bhqk', q, k) * intra[None]
    y = jnp.einsum('bhqk,bkhe->bqhe', scores, v)
    q_decay = jnp.exp(lg * (idx + 1.0)).T.astype(q.dtype)
    y = y + jnp.einsum('bqhd,bhde->bqhe', q * q_decay[None, :, :, None], s_state)
    k_decay = jnp.exp(lg * (l - 1.0 - idx)).T.astype(q.dtype)
    chunk_decay = jnp.exp(log_gamma * l).astype(q.dtype)[None, :, None, None]
    s_new = chunk_decay * s_state + jnp.einsum('bkhd,bkhe->bhde', k * k_decay[None, :, :, None], v)
    return s_new, y


def retention_layer(xp, xs, st, norm_g, w_in, w_out):
    log_gamma = jnp.log1p(-jnp.exp2(-5.0 - jnp.arange(B_HEADS, dtype=jnp.float32)))

    def project(x, pos):
        b, l = x.shape[0], x.shape[1]
        z = rmsnorm(x, norm_g, NORM_EPS) @ w_in
        q = z[..., :B_QK].reshape(b, l, B_HEADS, B_DK)
        k = z[..., B_QK:2 * B_QK].reshape(b, l, B_HEADS, B_DK)
        v = z[..., 2 * B_QK:2 * B_QK + B_V].reshape(b, l, B_HEADS, B_DV)
        g = z[..., 2 * B_QK + B_V:]
        ang = xpos_angles(pos)
        return rotate(q, ang), rotate(k, ang) * (B_DK ** -0.5), v, g

    def finish(x, o, g):
        b, l = x.shape[0], x.shape[1]
        o = head_rmsnorm(o, NORM_EPS).reshape(b, l, B_V)
        return x + (o * jax.nn.silu(g)) @ w_out

    b, s_len = xp.shape[0], xp.shape[1]
    n_chunks = s_len // CHUNK
    qp, kp, vp, gp = project(xp, jnp.arange(s_len, dtype=jnp.float32))
    to_chunks = lambda a: a.reshape(b, n_chunks, CHUNK, *a.shape[2:]).swapaxes(0, 1)
    s0 = jnp.zeros((b, B_HEADS, B_DK, B_DV), qp.dtype)
    s_final, yc = lax.scan(lambda s, xs_: retention_chunk(s, xs_[0], xs_[1], xs_[2], log_gamma),
                           s0, (to_chunks(qp), to_chunks(kp), to_chunks(vp)))
    op = yc.swapaxes(0, 1).reshape(b, s_len, B_HEADS, B_DV)
    yp = finish(xp, op, gp)

    qs, ks, vs, gs = project(xs, PAST_LEN + jnp.arange(xs.shape[1], dtype=jnp.float32))
    s_new, os_ = retention_chunk(st.astype(qs.dtype), qs, ks, vs, log_gamma)
    ys = finish(xs, os_, gs)
    return yp, ys, s_final, s_new


def setup_inputs(seed: int = 0) -> dict:
    key = jax.random.key(seed)
    ks = jax.random.split(key, 20)
    nrm = jax.random.normal
    f32 = jnp.float32
    return {
        'x_prompt': nrm(ks[0], (BATCH, SEQ, D_MODEL), f32),
        'x_sample': nrm(ks[1], (DEC_BATCH, DEC_SEQ, D_MODEL), f32),
        'cache_k_a': nrm(ks[2], (N_LAYERS_A, DEC_BATCH, PAST_LEN, 2 * A_HEADS, A_HEAD_DIM), f32),
        'cache_v_a': nrm(ks[3], (N_LAYERS_A, DEC_BATCH, PAST_LEN, A_HEADS, 2 * A_HEAD_DIM), f32),
        'state_ret': 0.5 * nrm(ks[4], (N_LAYERS_B, DEC_BATCH, B_HEADS, B_DK, B_DV), f32),
        'norm_a': 1.0 + 0.02 * nrm(ks[5], (N_LAYERS_A, D_MODEL), f32),
        'w_in_a': nrm(ks[6], (N_LAYERS_A, D_MODEL, A_IN), f32) * D_MODEL ** -0.5,
        'lambda_q1': 0.1 * nrm(ks[7], (N_LAYERS_A, A_HEAD_DIM), f32),
        'lambda_k1': 0.1 * nrm(ks[8], (N_LAYERS_A, A_HEAD_DIM), f32),
        'lambda_q2': 0.1 * nrm(ks[9], (N_LAYERS_A, A_HEAD_DIM), f32),
        'lambda_k2': 0.1 * nrm(ks[10], (N_LAYERS_A, A_HEAD_DIM), f32),
        'subln_a': 1.0 + 0.02 * nrm(ks[11], (N_LAYERS_A, 2 * A_HEAD_DIM), f32),
        'w_out_a': nrm(ks[12], (N_LAYERS_A, A_V, D_MODEL), f32) * A_V ** -0.5,
        'norm_b': 1.0 + 0.02 * nrm(ks[13], (N_LAYERS_B, D_MODEL), f32),
        'w_in_b': nrm(ks[14], (N_LAYERS_B, D_MODEL, B_IN), f32) * D_MODEL ** -0.5,
        'w_out_b': nrm(ks[15], (N_LAYERS_B, B_V, D_MODEL), f32) * B_V ** -0.5,
        'norm_final': 1.0 + 0.02 * nrm(ks[16], (D_MODEL,), f32),
    }


def reference(x_prompt, x_sample, cache_k_a, cache_v_a, state_ret, norm_a, w_in_a,
              lambda_q1, lambda_k1, lambda_q2, lambda_k2, subln_a, w_out_a,
              norm_b, w_in_b, w_out_b, norm_final):
    xp, xs = x_prompt, x_sample
    k_p, v_p, k_s, v_s, sr_p, sr_s = [], [], [], [], [], []
    for i in range(DEPTH):
        j = i // 2
        if i % 2 == 0:
            xp, xs, kp, vp, ks_, vs_ = diff_layer(
                xp, xs, cache_k_a[j], cache_v_a[j], norm_a[j], w_in_a[j],
                lambda_q1[j], lambda_k1[j], lambda_q2[j], lambda_k2[j], subln_a[j], w_out_a[j], i)
            k_p.append(kp)
            v_p.append(vp)
            k_s.append(ks_)
            v_s.append(vs_)
        else:
            xp, xs, sp, ss = retention_layer(xp, xs, state_ret[j], norm_b[j], w_in_b[j], w_out_b[j])
            sr_p.append(sp)
            sr_s.append(ss)
    y_prompt = rmsnorm(xp, norm_final, NORM_EPS)
    y_sample = rmsnorm(xs, norm_final, NORM_EPS)
    new_k_a_prompt = jnp.stack(k_p)
    new_v_a_prompt = jnp.stack(v_p)
    new_state_ret_prompt = jnp.stack(sr_p)
    new_k_a_sample = jnp.stack(k_s)
    new_v_a_sample = jnp.stack(v_s)
    new_state_ret_sample = jnp.stack(sr_s)
    return (y_prompt, y_sample, new_k_a_prompt, new_v_a_prompt, new_state_ret_prompt,
            new_k_a_sample, new_v_a_sample, new_state_ret_sample)
```

```python
import math
from contextlib import ExitStack

import numpy as np
import concourse.bass as bass
import concourse.mybir as mybir
from concourse.bass_utils import run_bass_kernel_spmd

F32 = mybir.dt.float32
BF = mybir.dt.bfloat16
AF = mybir.ActivationFunctionType
ALU = mybir.AluOpType

D = 2048
SEQ = 2048
NTP = 2112
NCORES = 8
NS_DMA = 8
SIG_LIMIT = 24000
ENGS = ["pe", "act", "dve", "pool", "sp"]


class Op:
    __slots__ = ("eng", "fn", "deps", "is_dma", "sig", "need_sig")

    def __init__(self, eng, fn, is_dma):
        self.eng = eng
        self.fn = fn
        self.is_dma = is_dma
        self.deps = []
        self.sig = None
        self.need_sig = False


class Sched:
    def __init__(self):
        self.q = {e: [] for e in ENGS}
        self.lastw = {}
        self.readers = {}
        self.dma_hist = {e: [] for e in ENGS}
        self.nops = 0

    def add(self, eng, fn, reads=(), writes=(), dma=False):
        op = Op(eng, fn, dma)
        deps = {}
        for k in reads:
            w = self.lastw.get(k)
            if w is not None:
                deps[w] = "raw"
            if k[0] == "p" and k[1] in "bt":
                rd = self.readers.get(k)
                if rd:
                    for e2, o2 in rd.items():
                        if e2 != "dma" and e2 != eng:
                            deps.setdefault(o2, "rar")
        for k in writes:
            w = self.lastw.get(k)
            if w is not None:
                deps.setdefault(w, "waw")
            rd = self.readers.get(k)
            if rd:
                for e2, o2 in rd.items():
                    if e2 == "dma":
                        for o3 in o2:
                            deps.setdefault(o3, "war")
                    else:
                        deps.setdefault(o2, "war")
        if dma:
            hist = self.dma_hist[eng]
            if len(hist) >= NS_DMA:
                deps[hist[len(hist) - NS_DMA]] = "raw"
            hist.append(op)
        for d, kind in deps.items():
            if d is op:
                continue
            if (not d.is_dma) and (not dma) and d.eng == eng:
                if eng == "pe" or kind != "raw":
                    continue
            op.deps.append(d)
            d.need_sig = True
        for k in reads:
            rd = self.readers.setdefault(k, {})
            if dma:
                rd.setdefault("dma", []).append(op)
            else:
                rd[eng] = op
        for k in writes:
            self.lastw[k] = op
            self.readers[k] = {}
        self.q[eng].append(op)
        self.nops += 1
        return op

    def assign(self, csems, dsems):
        self.final_dma = []
        for e in ENGS:
            cnt = 0
            epoch = 0
            dcount = [0] * NS_DMA
            di = 0
            for op in self.q[e]:
                if op.is_dma:
                    j = di % NS_DMA
                    dcount[j] += 16
                    op.sig = (dsems[e][j], dcount[j])
                    di += 1
                elif op.need_sig:
                    cnt += 1
                    if cnt > SIG_LIMIT:
                        epoch += 1
                        cnt = 1
                    op.sig = (csems[e][epoch], cnt)
            for j in range(NS_DMA):
                if dcount[j]:
                    self.final_dma.append((dsems[e][j], dcount[j]))

    def emit(self, e, engine, final=False):
        waited = {}
        for op in self.q[e]:
            for d in op.deps:
                sem, val = d.sig
                key = id(sem)
                if waited.get(key, 0) >= val:
                    continue
                engine.wait_ge(sem, val)
                waited[key] = val
            ins = op.fn(engine)
            if op.is_dma:
                ins.then_inc(op.sig[0], 16)
            elif op.need_sig:
                ins.then_inc(op.sig[0], 1)
        if final:
            for sem, val in self.final_dma:
                engine.wait_ge(sem, val)


def _tables():
    f32 = np.float32
    pos = np.zeros((17, 128), f32)
    for t in range(16):
        pos[t] = t * 128 + np.arange(128)
    for p in range(64):
        if (p % 32) < 16:
            pos[16, p] = 1024 + (p % 32)
    inv = np.power(f32(500000.0), -(np.arange(0, 16, 2, dtype=f32) / f32(16))).astype(f32)
    ang = (pos[:, :, None] * inv[None, None, :]).astype(f32).astype(np.float64)
    ca = np.cos(ang).astype(f32)
    sa = np.sin(ang).astype(f32)
    ropeAc = np.ascontiguousarray(np.broadcast_to(ca.transpose(1, 0, 2)[:, :, None, :], (128, 17, 4, 8))).reshape(128, 17 * 32)
    ropeAs = np.ascontiguousarray(np.broadcast_to(sa.transpose(1, 0, 2)[:, :, None, :], (128, 17, 4, 8))).reshape(128, 17 * 32)
    invb = (f32(1.0) / np.power(f32(10000.0), np.linspace(0.0, 1.0, 128, dtype=f32))).astype(f32)
    angb = (pos[:, :, None] * invb[None, None, :]).astype(f32).astype(np.float64)
    cb = np.cos(angb).astype(f32)
    sb = np.sin(angb).astype(f32)
    ropeB = np.zeros((17, 128, 2, 2, 128), f32)
    ropeB[:, :, 0, 0] = cb
    ropeB[:, :, 0, 1] = cb
    ropeB[:, :, 1, 0] = sb
    ropeB[:, :, 1, 1] = sb
    ropeB = ropeB.reshape(17, 128, 512)
    lg = np.log1p(-np.exp2(-5.0 - np.arange(8, dtype=f32))).astype(f32).astype(np.float64)
    idx = np.arange(128)
    q = idx[None, :]
    k = idx[:, None]
    same = (q // 64) == (k // 64)
    cross = (q // 64 == 1) & (k // 64 == 0)
    DTp = np.zeros((8, 128, 128), np.float64)
    for h in range(8):
        e = np.where(same, np.abs(q - k), np.where(cross, q - k, 0)).astype(np.float64)
        m = (same | cross).astype(np.float64)
        DTp[h] = m * np.exp(lg[h] * (e - (q + 1))) / 16.0
    DTs = np.zeros((64, 8, 64), np.float64)
    gqp = np.zeros((128, 8)); epsp = np.zeros((128, 8)); kdp = np.zeros((128, 8))
    gqs = np.ones((128, 8)); epss = np.full((128, 8), 1e-6); kds = np.zeros((128, 8))
    for h in range(8):
        gqp[:, h] = np.exp(lg[h] * (idx + 1))
        kdp[:, h] = np.exp(lg[h] * (127 - idx)) / 16.0
        for b in range(2):
            for i in range(16):
                gqs[b * 32 + i, h] = np.exp(lg[h] * (i + 1))
                kds[b * 32 + i, h] = np.exp(lg[h] * (15 - i)) / 16.0
                for j in range(16):
                    DTs[b * 32 + j, h, b * 32 + i] = np.exp(lg[h] * (abs(i - j) - (i + 1))) / 16.0
    epsp = 1e-6 / (gqp * gqp)
    epss = 1e-6 / (gqs * gqs)
    small = np.concatenate([gqp, epsp, kdp, gqs, epss, kds], axis=1).astype(f32)
    cd128 = [float(np.exp(lg[h] * 128)) for h in range(8)]
    cd16 = [float(np.exp(lg[h] * 16)) for h in range(8)]
    return dict(ropeAc=ropeAc.astype(f32), ropeAs=ropeAs.astype(f32), ropeB=ropeB,
                DTp=DTp.astype(f32), DTs=np.ascontiguousarray(DTs.reshape(64, 512)).astype(f32),
                small=small), cd128, cd16


_TAB, _CD128, _CD16 = _tables()


def build_program():
    nc = bass.Bass("TRN2", target_bir_lowering=False)

    def din(name, shape, dtype=F32):
        return nc.dram_tensor(name, list(shape), dtype, kind="ExternalInput").ap()

    def dout(name, shape, dtype=F32):
        return nc.dram_tensor(name, list(shape), dtype, kind="ExternalOutput").ap()

    def dscr(name, shape, dtype):
        return nc.dram_tensor(name, list(shape), dtype, kind="Internal").ap()

    xp = din("xp", [2, SEQ, D])
    xs = din("xs", [64, D])
    ck = din("ck", [2, 1024, D])
    cv = din("cv", [2, 1024, D])
    st = din("st", [2, 8, 256, 512])
    wa = din("wa", [16, 128, 16, 512])
    woa = din("woa", [128, 16, 2048])
    wb = din("wb", [8, 3, 128, 16, 512])
    wob = din("wob", [2, 128, 16, 2048])
    gA_d = din("gA", [128, 16])
    gB_d = din("gB", [128, 16])
    gF_d = din("gF", [1, D])
    lam_d = din("lam4", [1, 256])
    subln_d = din("subln", [128, 1])
    ropeAc_d = din("ropeAc", [128, 17 * 32])
    ropeAs_d = din("ropeAs", [128, 17 * 32])
    ropeB_d = din("ropeB", [17, 128, 512])
    DTp_d = din("DTp", [8, 128, 128])
    DTs_d = din("DTs", [64, 512])
    small_d = din("small", [128, 48])

    y_p = dout("y_p", [2, SEQ, D])
    y_s = dout("y_s", [64, D])
    nk_p = dout("nk_p", [2, SEQ, D])
    nv_p = dout("nv_p", [2, SEQ, D])
    ns_p = dout("ns_p", [2, 8, 256, 512])
    nk_s = dout("nk_s", [64, D])
    nv_s = dout("nv_s", [64, D])
    ns_s = dout("ns_s", [2, 8, 256, 512])

    scr0 = dscr("scr0", [2, 17, 128, 16, 128], BF)
    scr1 = dscr("scr1", [2, 17, 128, 32, 128], BF)
    x1s = dscr("x1s", [2, 17, 128, D], F32)
    x2s = dscr("x2s", [2, 17, 128, D], F32)

    S = Sched()

    with ExitStack() as es:
        def sb(name, shape, dtype):
            return es.enter_context(nc.sbuf_tensor("sb_" + name, list(shape), dtype))

        def pst(name, shape, dtype):
            return es.enter_context(nc.psum_tensor("ps_" + name, list(shape), dtype))

        RING = 3 * 8192
        BIG = sb("BIG", [128, RING + 16 * NTP], BF)
        wring = [BIG[:, i * 8192:(i + 1) * 8192].rearrange("p (k n) -> p k n", k=16) for i in range(3)]
        xnT = BIG[:, RING:RING + 16 * NTP].rearrange("p (k n) -> p k n", k=16)
        Wo = BIG[:, RING:RING + 16 * 2048].rearrange("p (k n) -> p k n", k=16)
        ogt = [BIG[:, 8192 * (1 + i):8192 * (1 + i) + 2048].rearrange("p (k n) -> p k n", k=16) for i in range(2)]
        gFb = BIG[:, 0:4096].bitcast(F32)
        xt = [sb(f"xt{i}", [128, D], F32) for i in range(2)]
        xnb = sb("xnb", [128, D], BF)
        Fall = sb("Fall", [128, 6 * 512], F32)
        Ft = [Fall[:, i * 512:(i + 1) * 512] for i in range(6)]
        xnb2 = Fall[:, 0:1024].bitcast(BF)
        zf = [sb(f"zf{i}", [128, 384], F32) for i in range(4)]
        qkT = sb("qkT", [128, 4, NTP], BF)
        vk = sb("vk", [128, 17 * 256], BF)
        Vh = vk[:, 0:17 * 128].rearrange("p (t e) -> p t e", t=17)
        kd = vk[:, :].rearrange("p (t e) -> p t e", t=17)
        Sst = sb("Sst", [128, 2, 512], F32)
        Sbt = sb("Sb", [128, 1024], BF)
        Sb = Sbt[:, :].rearrange("p (k n) -> p k n", k=2)
        ckT = Sbt
        pT = [sb(f"pT{i}", [128, 512], BF) for i in range(3)]
        bfw = [sb(f"bfw{i}", [128, 512], BF) for i in range(4)]
        vtt = sb("vt", [128, 1024], BF)
        vt = [vtt[:, i * 512:(i + 1) * 512] for i in range(2)]
        ckst = vtt[:, :].rearrange("p (k n) -> p k n", k=8)
        cvh = sb("cvh", [128, 8, 128], BF)
        xt0b = xt[0][:, :].bitcast(BF)
        xck = [xt0b[:, (2 * b) * 1024:(2 * b + 1) * 1024].rearrange("p (k n) -> p k n", k=8) for b in range(2)]
        xcv = [xt0b[:, (2 * b + 1) * 1024:(2 * b + 2) * 1024].rearrange("p (k n) -> p k n", k=8) for b in range(2)]
        StA = [xt[1][:, b * 1024:(b + 1) * 1024].rearrange("p (k n) -> p k n", k=2) for b in range(2)]
        ropeAc = sb("ropeAc", [128, 17 * 32], F32)
        ropeAs = sb("ropeAs", [128, 17 * 32], F32)
        rb = [sb(f"rb{i}", [128, 512], F32) for i in range(2)]
        DT = [sb(f"DT{i}", [128, 128], F32) for i in range(2)]
        DTs = sb("DTs", [64, 512], F32)
        small = sb("small", [128, 48], F32)
        gA = sb("gA", [128, 16], F32)
        gB = sb("gB", [128, 16], F32)
        lam4 = Ft[1][:, 0:256]
        lamw = sb("lamw", [128, 8], F32)
        cvec = sb("cvec", [128, 1], F32)
        ssq = sb("ssq", [128, 4], F32)
        rstd = sb("rstd", [128, 4], F32)
        rt = sb("rt", [128, 4, 32], F32)
        identf = Ft[2][:, 0:128]
        ident = sb("ident", [128, 128], BF)
        ones = sb("ones", [128, 128], BF)
        dummy = sb("dummy", [128, 4], F32)
        mhalf = sb("mhalf", [128, 1], F32)

        pb = [pst(f"pb{i}", [128, 512], F32) for i in range(7)]
        pt = pst("pt", [128, 1024], BF)
        pTa = [pT[0][:, :], pT[1][:, :], pT[2][:, :], rb[0][:, :].bitcast(BF)[:, 0:512], rb[1][:, :].bitcast(BF)[:, 0:512]]
        pTk = ["pT0", "pT1", "pT2", "rb0", "rb1"]
        sbank = [pb[0][:, :], pb[1][:, :], pb[6][:, :], pt[:, :].bitcast(F32)]
        sbk = ["pb0", "pb1", "pb6", "pt"]

        csems = {e: [es.enter_context(nc.semaphore(f"c_{e}{i}")) for i in range(4)] for e in ENGS}
        dsems = {e: [es.enter_context(nc.semaphore(f"d_{e}{i}")) for i in range(NS_DMA)] for e in ("sp", "pool", "act")}
        dsems["dve"] = dsems["sp"]
        dsems["pe"] = dsems["sp"]

        def MM(out, lhsT, rhs, start, stop, r, w):
            S.add("pe", lambda e: e.matmul(out, lhsT=lhsT, rhs=rhs, start=start, stop=stop), r, w)

        def TR(out, in_, rows, r, w):
            S.add("pe", lambda e: e.transpose(out, in_, ident[0:rows, 0:rows]), r, w)

        def ACT(out, in_, func, r, w, scale=1.0, accum=None):
            def f(e):
                kw = dict(out=out, in_=in_, func=func, scale=scale)
                if accum is not None:
                    kw["accum_out"] = accum
                return e.activation(**kw)
            S.add("act", f, r, w)

        def VOP(eng, name, r, w, **kw):
            S.add(eng, lambda e: getattr(e, name)(**kw), r, w)

        def DMA(qn, out, in_, r, w):
            S.add(qn, lambda e: e.dma_start(out=out, in_=in_), r, w, dma=True)

        state = dict(ab=0, zi=0, pi=0, wi=0, ri=0, oi=0, s3=0, s4=0, p5=0)

        def next_ab():
            state["ab"] ^= 1
            return state["ab"]

        def rot(name, n):
            v = state[name]
            state[name] = (v + 1) % n
            return v

        XNT_KEYS = [f"xnT{t}" for t in range(17)]

        VOP("pool", "memset", [], ["identf", "F2"], ap=identf, constant=0.0)
        S.add("pool", lambda e: e.affine_select(out=identf, in_=identf, pattern=[[-1, 128]],
                                                compare_op=ALU.not_equal, fill=1.0, base=0,
                                                channel_multiplier=1), ["identf"], ["identf"])
        VOP("pool", "tensor_copy", ["identf", "F2"], ["ident"], out=ident[:], in_=identf)
        VOP("pool", "memset", [], ["ones"], ap=ones[:], constant=1.0)
        VOP("pool", "memset", [], ["dummy"], ap=dummy[:], constant=0.0)
        VOP("pool", "memset", [], ["mhalf"], ap=mhalf[:], constant=-0.5)
        DMA("sp", ropeAc[:], ropeAc_d, [], ["ropeA"])
        DMA("sp", ropeAs[:], ropeAs_d, [], ["ropeA"])
        DMA("sp", DTs[:], DTs_d, [], ["DTs"])
        DMA("sp", small[:], small_d, [], ["small"])
        DMA("sp", gA[:], gA_d, [], ["gA"])
        DMA("sp", gB[:], gB_d, [], ["gB"])
        DMA("sp", lam4, lam_d.broadcast_to([128, 256]), [], ["lam4", "F1"])
        DMA("sp", cvec[:], subln_d, [], ["cvec"])
        VOP("dve", "tensor_tensor", ["lam4", "F1"], ["lamw", "F0"], out=Ft[0][:, 0:64], in0=lam4[:, 0:64], in1=lam4[:, 64:128], op=ALU.mult)
        VOP("dve", "tensor_reduce", ["lamw", "F0"], ["lamw"], out=lamw[:, 0:1], in_=Ft[0][:, 0:64], axis=mybir.AxisListType.X, op=ALU.add)
        VOP("dve", "tensor_tensor", ["lam4", "F1"], ["lamw", "F0"], out=Ft[0][:, 64:128], in0=lam4[:, 128:192], in1=lam4[:, 192:256], op=ALU.mult)
        VOP("dve", "tensor_reduce", ["lamw", "F0"], ["lamw"], out=lamw[:, 1:2], in_=Ft[0][:, 64:128], axis=mybir.AxisListType.X, op=ALU.add)
        ACT(lamw[:, 2:4], lamw[:, 0:2], AF.Exp, ["lamw"], ["lamw"])
        VOP("dve", "tensor_tensor", ["lamw"], ["lamw"], out=lamw[:, 4:5], in0=lamw[:, 3:4], in1=lamw[:, 2:3], op=ALU.subtract)
        VOP("dve", "tensor_scalar", ["lamw"], ["nlam"], out=lamw[:, 5:6], in0=lamw[:, 4:5], scalar1=-0.2, scalar2=None, op0=ALU.add)
        nlam = lamw[:, 5:6]
        VOP("dve", "tensor_scalar", ["cvec"], ["cvec"], out=cvec[:], in0=cvec[:], scalar1=float(0.8 * math.sqrt(128.0)),
            scalar2=None, op0=ALU.mult)

        def phase_norm(src_fn, gcol, gkey, ntiles, srck=None):
            pt2 = pb[6][:, :].bitcast(BF)
            for t in range(ntiles):
                rows = 128 if t < 16 else 64
                b = t % 2
                xb = xnb[:, :] if b == 0 else xnb2
                xk = ["xnb"] if b == 0 else ["F0", "F1"]
                DMA("sp", xt[b][0:rows, :], src_fn(t), ([f"{srck}_{t}"] if srck else []), [f"xt{b}"])
                VOP("pool", "memset", [], [f"ssq{b}"], ap=ssq[0:rows, b:b + 1], constant=0.0)
                ACT(xb[0:rows, :], xt[b][0:rows, :], AF.Square, [f"xt{b}"], xk + [f"ssq{b}"], accum=ssq[0:rows, b:b + 1])
                VOP("dve", "tensor_scalar", [f"ssq{b}"], [f"rstd{b}"], out=rstd[0:rows, b:b + 1], in0=ssq[0:rows, b:b + 1],
                    scalar1=1.0 / D, scalar2=1e-6, op0=ALU.mult, op1=ALU.add)
                VOP("pool", "tensor_tensor", [f"rstd{b}", "mhalf"], [f"rstd{b}"], out=rstd[0:rows, b:b + 1], in0=rstd[0:rows, b:b + 1],
                    in1=mhalf[0:rows, 0:1], op=ALU.pow)
                VOP("dve", "tensor_scalar", [f"xt{b}", f"rstd{b}"], xk, out=xb[0:rows, :], in0=xt[b][0:rows, :],
                    scalar1=rstd[0:rows, b:b + 1], scalar2=None, op0=ALU.mult)
                for half in range(2):
                    bank = pt if half == 0 else pt2
                    bkey = "pt" if half == 0 else "pb6"
                    for j in range(8):
                        kc = half * 8 + j
                        TR(bank[0:128, j * 128:j * 128 + rows], xb[0:rows, kc * 128:(kc + 1) * 128], rows, xk + ["ident"], [bkey])
                    for j in range(8):
                        kc = half * 8 + j
                        dst = xnT[:, kc, t * 128:t * 128 + rows]
                        srcp = bank[:, j * 128:j * 128 + rows]
                        if half == 0:
                            ACT(dst, srcp, AF.Identity, [bkey, gkey], [f"xnT{t}"], scale=gcol[:, kc:kc + 1])
                        else:
                            VOP("dve", "tensor_scalar", [bkey, gkey], [f"xnT{t}"], out=dst, in0=srcp,
                                scalar1=gcol[:, kc:kc + 1], scalar2=None, op0=ALU.mult)

        def load_w(src):
            slot = rot("wi", 3)
            DMA("pool", wring[slot], src, [], [f"w{slot}"])
            return wring[slot], f"w{slot}"

        def TRANS0(keys):
            VOP("pool", "memset", [], keys, ap=dummy[:, 1:2], constant=0.0)

        XA_KEYS = ["xt0", "xck0", "xck1", "xcv0", "xcv1"]
        XB_KEYS = ["xt1", "StA0", "StA1"]
        pend_fin = [None]

        def flush_fin(nstage=2):
            if pend_fin[0] is not None:
                for _ in range(nstage):
                    if pend_fin[0]:
                        pend_fin[0].pop(0)()
                if not pend_fin[0]:
                    pend_fin[0] = None

        def phase_attn(s):
            TRANS0(XA_KEYS)
            wnext = load_w(wa[0])
            for h in range(16):
                W, wkey = wnext
                if h + 1 < 16:
                    wnext = load_w(wa[h + 1])
                hc = slice(h * 128, (h + 1) * 128)

                def proj_tile(t, rows):
                    pbk = next_ab()
                    for kc in range(16):
                        MM(pb[pbk][0:rows, 0:384], xnT[:, kc, t * 128:t * 128 + rows], W[:, kc, 0:384], kc == 0, kc == 15,
                           [f"xnT{t}", wkey], [f"pb{pbk}"])
                    zi = rot("zi", 4)
                    z = zf[zi]
                    ACT(z[0:rows, :], pb[pbk][0:rows, 0:384], AF.Identity, [f"pb{pbk}"], [f"zf{zi}"])
                    ACT(Vh[0:rows, t, :], pb[pbk][0:rows, 256:384], AF.Identity, [f"pb{pbk}"], [f"vk{t}"])
                    zv = z[0:rows, 0:256].rearrange("p (s d) -> p s d", d=64)
                    x1 = zv[:, :, 0:8]
                    x2 = zv[:, :, 8:16]
                    cos = ropeAc[0:rows, t * 32:(t + 1) * 32].rearrange("p (s d) -> p s d", d=8)
                    sin = ropeAs[0:rows, t * 32:(t + 1) * 32].rearrange("p (s d) -> p s d", d=8)
                    rv = [rt[0:rows, i, :].rearrange("p (s d) -> p s d", d=8) for i in range(4)]
                    rk = [f"zf{zi}", "ropeA"]
                    VOP("dve", "tensor_tensor", rk, ["rt"], out=rv[0], in0=x1, in1=cos, op=ALU.mult)
                    VOP("dve", "tensor_tensor", rk, ["rt"], out=rv[1], in0=x2, in1=sin, op=ALU.mult)
                    VOP("dve", "tensor_tensor", rk, ["rt"], out=rv[2], in0=x1, in1=sin, op=ALU.mult)
                    VOP("dve", "tensor_tensor", rk, ["rt"], out=rv[3], in0=x2, in1=cos, op=ALU.mult)
                    VOP("dve", "tensor_tensor", ["rt"], [f"zf{zi}"], out=x1, in0=rv[0], in1=rv[1], op=ALU.subtract)
                    VOP("dve", "tensor_tensor", ["rt"], [f"zf{zi}"], out=x2, in0=rv[2], in1=rv[3], op=ALU.add)
                    qb = bfw[2 + (t % 2)]
                    qkey = f"bfw{2 + (t % 2)}"
                    VOP("dve", "tensor_copy", [f"zf{zi}"], [qkey], out=qb[0:rows, 0:256], in_=z[0:rows, 0:256])
                    if t < 16:
                        DMA("sp", nk_p[s, t * 128:(t + 1) * 128, hc], z[:, 128:256], [f"zf{zi}"], [])
                        DMA("sp", nv_p[s, t * 128:(t + 1) * 128, hc], z[:, 256:384], [f"zf{zi}"], [])
                    else:
                        DMA("sp", nk_s[0:64, hc], z[0:64, 128:256], [f"zf{zi}"], [])
                        DMA("sp", nv_s[0:64, hc], z[0:64, 256:384], [f"zf{zi}"], [])

                def proj_tile2(t, rows):
                    qb = bfw[2 + (t % 2)]
                    qkey = f"bfw{2 + (t % 2)}"
                    TR(pt[0:128, 0:rows], qb[0:rows, 0:128], rows, [qkey, "ident"], ["pt"])
                    TR(pt[0:128, 128:128 + rows], qb[0:rows, 128:256], rows, [qkey, "ident"], ["pt"])
                    ACT(qkT[:, 0:2, t * 128:t * 128 + rows],
                        pt[:, 0:256].rearrange("p (a n) -> p a n", a=2)[:, :, 0:rows], AF.Identity, ["pt"], [f"qkT{t}"])

                def gate(col0, n, tiles):
                    pbk = next_ab()
                    for kc in range(16):
                        MM(pb[pbk][:, 0:n], W[:, kc, 384:512], xnT[:, kc, col0:col0 + n], kc == 0, kc == 15,
                           [f"xnT{t}" for t in tiles] + [wkey], [f"pb{pbk}"])
                    ACT(Ft[3][:, 0:n], pb[pbk][:, 0:n], AF.Exp, [f"pb{pbk}"], ["F3"], scale=-1.0)
                    ACT(Ft[5][:, 0:n], pb[pbk][:, 0:n], AF.Identity, [f"pb{pbk}"], ["F5"])
                    VOP("dve", "tensor_scalar", ["F3"], ["F3"], out=Ft[3][:, 0:n], in0=Ft[3][:, 0:n], scalar1=1.0, scalar2=None, op0=ALU.add)
                    ACT(Ft[3][:, 0:n], Ft[3][:, 0:n], AF.Ln, ["F3"], ["F3"])
                    ACT(Ft[3][:, 0:n], Ft[3][:, 0:n], AF.Exp, ["F3"], ["F3"], scale=-1.0)
                    VOP("dve", "tensor_tensor", ["F3", "F5"], ["F4"], out=Ft[4][:, 0:n], in0=Ft[3][:, 0:n],
                        in1=Ft[5][:, 0:n], op=ALU.mult)

                def finish(n, dst_ap, skeys):
                    flush_fin()
                    finish_a(n, dst_ap, skeys)
                    pend_fin[0] = [lambda: finish_b(n, dst_ap, skeys)]

                def finish_a(n, dst_ap, skeys):
                    f0, f1, f2 = Ft[0], Ft[1], Ft[2]
                    ACT(f0[:, 0:n], pb[3][:, 0:n], AF.Ln, ["pb3"], ["F0"])
                    ACT(f0[:, 0:n], f0[:, 0:n], AF.Exp, ["F0"], ["F0"], scale=-1.0)
                    VOP("dve", "tensor_tensor", ["pb2", "F0"], ["F1"], out=f1[:, 0:n], in0=pb[2][:, 0:n], in1=f0[:, 0:n], op=ALU.mult)
                    ACT(f2[:, 0:n], pb[5][:, 0:n], AF.Ln, ["pb5"], ["F2"])
                    ACT(f2[:, 0:n], f2[:, 0:n], AF.Exp, ["F2"], ["F2"], scale=-1.0)
                    VOP("dve", "tensor_tensor", ["pb4", "F2"], ["F2"], out=f2[:, 0:n], in0=pb[4][:, 0:n], in1=f2[:, 0:n], op=ALU.mult)
                    VOP("dve", "scalar_tensor_tensor", ["F1", "F2", "nlam"], ["F1"], out=f1[:, 0:n], in0=f2[:, 0:n], scalar=nlam,
                        in1=f1[:, 0:n], op0=ALU.mult, op1=ALU.add)
                    ACT(bfw[0][:, 0:n], f1[:, 0:n], AF.Square, ["F1"], ["bfw0"])

                def finish_b(n, dst_ap, skeys):
                    f0, f1, f2 = Ft[0], Ft[1], Ft[2]
                    MM(pb[6][:, 0:n], ones[:, :], bfw[0][:, 0:n], True, True, ["bfw0", "ones"], ["pb6"])
                    VOP("dve", "tensor_scalar", ["pb6"], ["F0"], out=f0[:, 0:n], in0=pb[6][:, 0:n], scalar1=128.0 * 1e-5, scalar2=None,
                        op0=ALU.add)
                    ACT(f0[:, 0:n], f0[:, 0:n], AF.Ln, ["F0"], ["F0"])
                    ACT(f0[:, 0:n], f0[:, 0:n], AF.Exp, ["F0"], ["F0"], scale=-0.5)
                    VOP("dve", "tensor_tensor", ["F1", "F0"], ["F1"], out=f1[:, 0:n], in0=f1[:, 0:n], in1=f0[:, 0:n], op=ALU.mult)
                    VOP("dve", "scalar_tensor_tensor", ["F1", "F4", "cvec"], ["bfw1"], out=bfw[1][:, 0:n], in0=f1[:, 0:n],
                        scalar=cvec[:, 0:1], in1=Ft[4][:, 0:n], op0=ALU.mult, op1=ALU.mult)
                    if n != 512:
                        VOP("pool", "memset", [], ["bfw1"], ap=bfw[1][:, 16:32], constant=0.0)
                        VOP("pool", "memset", [], ["bfw1"], ap=bfw[1][:, 48:64], constant=0.0)
                        DMA("sp", dst_ap, bfw[1][:, 0:64], ["bfw1"], skeys)
                    else:
                        DMA("sp", dst_ap, bfw[1][:, :].rearrange("p (t n) -> p t n", t=4), ["bfw1"], skeys)

                def attn_group(G):
                    nblk = 4 * G + 4

                    def st_pair(i):
                        infos = []
                        qlo = max(i, 4 * G) * 128
                        qhi = (4 * G + 4) * 128
                        n = qhi - qlo
                        co = qlo - 4 * G * 128
                        banks = []
                        for sub in range(2):
                            r0 = sub * 64
                            bi = rot("s4", 4)
                            banks.append(bi)
                            MM(sbank[bi][:, 0:n], qkT[r0:r0 + 64, 1, i * 128:(i + 1) * 128], qkT[r0:r0 + 64, 0, qlo:qhi], True, True,
                               [f"qkT{i}"] + [f"qkT{tt}" for tt in range(qlo // 128, 4 * G + 4)], [sbk[bi]])
                        for sub in range(2):
                            bi = banks[sub]
                            pi = rot("p5", 5)
                            p = pTa[pi]
                            ACT(p[:, 0:n], sbank[bi][:, 0:n], AF.Exp, [sbk[bi]], [pTk[pi]], scale=0.125)
                            if i >= 4 * G:
                                VOP("pool", "memset", [], [pTk[pi]], ap=p[64:128, 0:64], constant=0.0)
                            infos.append((sub, i, n, co, pi))
                        return infos

                    def pv_pair(infos):
                        for (sub, i, n, co, pi) in infos:
                            p = pTa[pi]
                            O = pb[2 + 2 * sub]
                            Dn = pb[3 + 2 * sub]
                            MM(O[:, co:512], Vh[:, i, :], p[:, 0:n], i == 0, i == nblk - 1, [f"vk{i}", pTk[pi]], [f"pb{2 + 2 * sub}"])
                            MM(Dn[:, co:512], ones[:, :], p[:, 0:n], i == 0, i == nblk - 1, ["ones", pTk[pi]], [f"pb{3 + 2 * sub}"])

                    pend = []
                    for i in range(nblk):
                        pend.append(st_pair(i))
                        if len(pend) > 1:
                            pv_pair(pend.pop(0))
                    while pend:
                        pv_pair(pend.pop(0))
                    finish(512, scr0[s, 4 * G:4 * G + 4, :, h, :].rearrange("t p n -> p t n"), [f"s0_{tt}" for tt in range(4 * G, 4 * G + 4)])

                def sample_prefetch():
                    for b in range(2):
                        DMA("pool", xck[b], ck[b].rearrange("(kb p) c -> p kb c", p=128)[:, :, hc], [], [f"xck{b}"])
                        DMA("pool", xcv[b], cv[b].rearrange("(kb p) c -> p kb c", p=128)[:, :, hc], [], [f"xcv{b}"])

                def sample_attn():
                    for b in range(2):
                        ckst = xck[b]
                        cvh = xcv[b]
                        for kb in range(8):
                            TR(pt[:, kb * 128:(kb + 1) * 128], ckst[:, kb, :], 128, [f"xck{b}", "ident"], ["pt"])
                        ACT(ckT[:, :], pt[:, :], AF.Identity, ["pt"], ["Sb"])
                        c0 = b * 32
                        for sub in range(2):
                            r0 = sub * 64
                            O = pb[2 + 2 * sub]
                            Dn = pb[3 + 2 * sub]
                            okey = f"pb{2 + 2 * sub}"
                            dkey = f"pb{3 + 2 * sub}"
                            pbk = next_ab()
                            qs = qkT[r0:r0 + 64, 0, 2048 + c0:2048 + c0 + 16]
                            for kb in range(8):
                                MM(pb[pbk][:, kb * 16:(kb + 1) * 16], ckT[r0:r0 + 64, kb * 128:(kb + 1) * 128], qs, True, True,
                                   ["Sb", "qkT16"], [f"pb{pbk}"])
                            MM(pb[pbk][0:64, 128:192], qkT[r0:r0 + 64, 1, 2048:2112], qkT[r0:r0 + 64, 0, 2048:2112], True, True,
                               ["qkT16"], [f"pb{pbk}"])
                            pi = rot("pi", 3)
                            p = pT[pi]
                            ACT(p[:, 0:128], pb[pbk][:, 0:128], AF.Exp, [f"pb{pbk}"], [f"pT{pi}"], scale=0.125)
                            ACT(p[0:64, 128:192], pb[pbk][0:64, 128:192], AF.Exp, [f"pb{pbk}"], [f"pT{pi}"], scale=0.125)
                            for kb in range(8):
                                MM(O[:, c0:c0 + 16], cvh[:, kb, :], p[:, kb * 16:(kb + 1) * 16], kb == 0, False, [f"xcv{b}", f"pT{pi}"], [okey])
                            MM(O[:, c0:c0 + 16], Vh[c0:c0 + 16, 16, :], p[c0:c0 + 16, 128 + c0:128 + c0 + 16], False, True,
                               ["vk16", f"pT{pi}"], [okey])
                            for kb in range(8):
                                MM(Dn[:, c0:c0 + 16], ones[:, :], p[:, kb * 16:(kb + 1) * 16], kb == 0, False, ["ones", f"pT{pi}"], [dkey])
                            MM(Dn[:, c0:c0 + 16], ones[c0:c0 + 16, :], p[c0:c0 + 16, 128 + c0:128 + c0 + 16], False, True,
                               ["ones", f"pT{pi}"], [dkey])
                    finish(48, scr0[s, 16, :, h, 0:64], ["s0_16"])

                if s == 0:
                    sample_prefetch()
                for G in range(4):
                    t0 = 4 * G
                    proj_tile(t0, 128)
                    proj_tile(t0 + 1, 128)
                    flush_fin(1)
                    proj_tile2(t0, 128)
                    proj_tile(t0 + 2, 128)
                    flush_fin(1)
                    proj_tile2(t0 + 1, 128)
                    proj_tile(t0 + 3, 128)
                    proj_tile2(t0 + 2, 128)
                    gate(G * 512, 512, range(4 * G, 4 * G + 4))
                    proj_tile2(t0 + 3, 128)
                    attn_group(G)
                if s == 0:
                    proj_tile(16, 64)
                    flush_fin()
                    gate(2048, 64, [16])
                    proj_tile2(16, 64)
                    sample_attn()

            flush_fin()
            TRANS0(XA_KEYS)

        QK_KEYS = [f"qkT{t}" for t in range(17)]
        VK_KEYS = [f"vk{t}" for t in range(17)]
        qkflat = qkT[:, :, :].rearrange("p a n -> p (a n)")

        def TRANS(keys):
            VOP("pool", "memset", [], keys, ap=dummy[:, 0:1], constant=0.0)

        def phase_outproj_a(s, ntiles):
            WaL = BIG[:, 0:12 * 2048].rearrange("p (k n) -> p k n", k=12)
            WaH = qkflat[:, 0:4 * 2048].rearrange("p (k n) -> p k n", k=4)
            oga = [vk[:, b * 2048:(b + 1) * 2048].rearrange("p (k n) -> p k n", k=16) for b in range(2)]
            pt2 = pb[6][:, :].bitcast(BF)
            TRANS(["w0", "w1", "w2"] + QK_KEYS + VK_KEYS + ["WaL0", "WaL1", "WaL2", "WaL3", "WaH0", "WaH1", "WaH2", "WaH3", "oga0", "oga1"])
            for n in range(4):
                cs = slice(n * 512, (n + 1) * 512)
                DMA("pool", WaL[:, :, cs], woa[:, 0:12, cs], [], [f"WaL{n}"])
                DMA("pool", WaH[:, :, cs], woa[:, 12:16, cs], [], [f"WaH{n}"])
            pendB = []
            nxt_loads = []

            def loads_a(t):
                rows = 128 if t < 16 else 64
                b = t % 2
                xsrc = xp[s, t * 128:(t + 1) * 128, :] if t < 16 else xs[0:64, :]
                DMA("sp", oga[b], scr0[s, t], [f"s0_{t}"], [f"oga{b}"])
                DMA("sp", xt[b][0:rows, :], xsrc, [], [f"xt{b}"])
            loads_a(0)
            loads_a(1)
            for t in range(ntiles):
                rows = 128 if t < 16 else 64
                b = t % 2
                for n in range(4):
                    cs = slice(n * 512, (n + 1) * 512)
                    for kc in range(16):
                        rhs = WaL[:, kc, cs] if kc < 12 else WaH[:, kc - 12, cs]
                        MM(pb[n][0:rows, :], oga[b][:, kc, 0:rows], rhs, kc == 0, kc == 15,
                           [f"oga{b}", f"WaL{n}", f"WaH{n}"], [f"pb{n}"])
                    VOP("dve", "tensor_tensor", [f"pb{n}", f"xt{b}"], [f"xt{b}"], out=xt[b][0:rows, cs],
                        in0=pb[n][0:rows, :], in1=xt[b][0:rows, cs], op=ALU.add)
                DMA("sp", x1s[s, t, 0:rows, :], xt[b][0:rows, :], [f"xt{b}"], [f"x1_{t}"])
                while pendB:
                    pendB.pop(0)()
                nxt_loads.append(t + 2)
                xb = xnb[:, :] if b == 0 else xnb2
                xk = ["xnb"] if b == 0 else ["F0", "F1"]
                VOP("pool", "memset", [], [f"ssq{b}"], ap=ssq[0:rows, b:b + 1], constant=0.0)
                ACT(xb[0:rows, :], xt[b][0:rows, :], AF.Square, [f"xt{b}"], xk + [f"ssq{b}"], accum=ssq[0:rows, b:b + 1])
                VOP("dve", "tensor_scalar", [f"ssq{b}"], [f"rstd{b}"], out=rstd[0:rows, b:b + 1], in0=ssq[0:rows, b:b + 1],
                    scalar1=1.0 / D, scalar2=1e-6, op0=ALU.mult, op1=ALU.add)
                VOP("pool", "tensor_tensor", [f"rstd{b}", "mhalf"], [f"rstd{b}"], out=rstd[0:rows, b:b + 1], in0=rstd[0:rows, b:b + 1],
                    in1=mhalf[0:rows, 0:1], op=ALU.pow)
                VOP("dve", "tensor_scalar", [f"xt{b}", f"rstd{b}"], xk, out=xb[0:rows, :], in0=xt[b][0:rows, :],
                    scalar1=rstd[0:rows, b:b + 1], scalar2=None, op0=ALU.mult)
                def partB(t=t, rows=rows, xb=xb, xk=xk):
                    for half in range(2):
                        bank = pt if half == 0 else pt2
                        bkey = "pt" if half == 0 else "pb6"
                        for j in range(8):
                            kc = half * 8 + j
                            TR(bank[0:128, j * 128:j * 128 + rows], xb[0:rows, kc * 128:(kc + 1) * 128], rows, xk + ["ident"], [bkey])
                        for j in range(8):
                            kc = half * 8 + j
                            dst = xnT[:, kc, t * 128:t * 128 + rows]
                            srcp = bank[:, j * 128:j * 128 + rows]
                            if half == 0:
                                ACT(dst, srcp, AF.Identity, [bkey, "gB"], [f"xnT{t}"], scale=gB[:, kc:kc + 1])
                            else:
                                VOP("dve", "tensor_scalar", [bkey, "gB"], [f"xnT{t}"], out=dst, in0=srcp,
                                    scalar1=gB[:, kc:kc + 1], scalar2=None, op0=ALU.mult)
                pendB.append(partB)
                while nxt_loads:
                    tn = nxt_loads.pop(0)
                    if tn < ntiles:
                        loads_a(tn)
            while pendB:
                pendB.pop(0)()
            TRANS(["w0", "w1", "w2"] + QK_KEYS + VK_KEYS + ["WaL0", "WaL1", "WaL2", "WaL3", "WaH0", "WaH1", "WaH2", "WaH3", "oga0", "oga1"])

        def phase_outproj_b(s, ntiles):
            WbL = BIG[:, 0:28 * 2048].rearrange("p (k n) -> p k n", k=28)
            WbH = qkflat[:, 0:4 * 2048].rearrange("p (k n) -> p k n", k=4)
            ogb = [vk[:, 0:4096].rearrange("p (k n) -> p k n", k=32),
                   Fall[:, 0:2048].bitcast(BF).rearrange("p (k n) -> p k n", k=32)]
            ogk = [["ogb0"], ["F0", "F1", "F2", "F3"]]
            gFlo = xnb[:, :].bitcast(F32)
            gFhi = Sst[:, :, :].rearrange("p a n -> p (a n)")
            allk = (["w0", "w1", "w2"] + XNT_KEYS + QK_KEYS + VK_KEYS + ["xnb", "S", "F0", "F1", "F2", "F3", "ogb0", "gF"]
                    + [f"WbL{n}" for n in range(4)] + [f"WbH{n}" for n in range(4)])
            TRANS(allk)
            for n in range(4):
                cs = slice(n * 512, (n + 1) * 512)
                DMA("pool", WbL[:, 0:16, cs], wob[0][:, :, cs], [], [f"WbL{n}"])
                DMA("pool", WbL[:, 16:28, cs], wob[1][:, 0:12, cs], [], [f"WbL{n}"])
                DMA("pool", WbH[:, :, cs], wob[1][:, 12:16, cs], [], [f"WbH{n}"])
            DMA("sp", gFlo, gF_d[:, 0:1024].broadcast_to([128, 1024]), [], ["gF"])
            DMA("sp", gFhi, gF_d[:, 1024:2048].broadcast_to([128, 1024]), [], ["gF"])
            def loads_b(t):
                rows = 128 if t < 16 else 64
                b = t % 2
                DMA("sp", ogb[b], scr1[s, t], [f"s1_{t}"], ogk[b])
                DMA("sp", xt[b][0:rows, :], x1s[s, t, 0:rows, :], [f"x1_{t}"], [f"xt{b}"])
            loads_b(0)
            loads_b(1)
            for t in range(ntiles):
                rows = 128 if t < 16 else 64
                b = t % 2
                for n in range(4):
                    cs = slice(n * 512, (n + 1) * 512)
                    for kc in range(32):
                        rhs = WbL[:, kc, cs] if kc < 28 else WbH[:, kc - 28, cs]
                        MM(pb[n][0:rows, :], ogb[b][:, kc, 0:rows], rhs, kc == 0, kc == 31,
                           ogk[b] + [f"WbL{n}", f"WbH{n}"], [f"pb{n}"])
                    VOP("dve", "tensor_tensor", [f"pb{n}", f"xt{b}"], [f"xt{b}"], out=xt[b][0:rows, cs],
                        in0=pb[n][0:rows, :], in1=xt[b][0:rows, cs], op=ALU.add)
                    VOP("pool", "memset", [], [f"ssq{n}"], ap=ssq[0:rows, n:n + 1], constant=0.0)
                    ACT(bfw[n][0:rows, :], xt[b][0:rows, cs], AF.Square, [f"xt{b}"], [f"bfw{n}", f"ssq{n}"], accum=ssq[0:rows, n:n + 1])
                VOP("dve", "tensor_reduce", [f"ssq{n}" for n in range(4)], ["rstdf"], out=rstd[0:rows, 3:4], in_=ssq[0:rows, 0:4],
                    axis=mybir.AxisListType.X, op=ALU.add)
                VOP("dve", "tensor_scalar", ["rstdf"], ["rstdf"], out=rstd[0:rows, 3:4], in0=rstd[0:rows, 3:4],
                    scalar1=1.0 / D, scalar2=1e-6, op0=ALU.mult, op1=ALU.add)
                VOP("pool", "tensor_tensor", ["rstdf", "mhalf"], ["rstdf"], out=rstd[0:rows, 3:4], in0=rstd[0:rows, 3:4],
                    in1=mhalf[0:rows, 0:1], op=ALU.pow)
                VOP("dve", "scalar_tensor_tensor", [f"xt{b}", "rstdf", "gF"], [f"xt{b}"], out=xt[b][0:rows, 0:1024],
                    in0=xt[b][0:rows, 0:1024], scalar=rstd[0:rows, 3:4], in1=gFlo[0:rows, :], op0=ALU.mult, op1=ALU.mult)
                VOP("dve", "scalar_tensor_tensor", [f"xt{b}", "rstdf", "gF"], [f"xt{b}"], out=xt[b][0:rows, 1024:2048],
                    in0=xt[b][0:rows, 1024:2048], scalar=rstd[0:rows, 3:4], in1=gFhi[0:rows, :], op0=ALU.mult, op1=ALU.mult)
                dst = y_p[s, t * 128:(t + 1) * 128, :] if t < 16 else y_s[0:64, :]
                DMA("sp", dst, xt[b][0:rows, :], [f"xt{b}"], [])
                if t + 2 < ntiles:
                    loads_b(t + 2)
            TRANS(allk)

        def phase_ret(s):
            ntiles = 17 if s == 0 else 16
            TRANS0(XB_KEYS)
            GQ, EPQ, KDC = 0, 8, 16
            pend2 = [None]
            nxt = [load_w(wb[0, 0]), load_w(wb[0, 1]), load_w(wb[0, 2])]
            for h in range(8):
                (Wqk, kqk), (Wv, kv), (Wg, kg) = nxt
                nxt = [None, None, None]
                dti = h % 2
                DMA("sp", DT[dti][:, :], DTp_d[h], [], [f"DT{dti}"])

                VOP("pool", "memset", [], ["S"], ap=Sst[:, :, :], constant=0.0)
                VOP("pool", "memset", [], ["Sb"], ap=Sbt[:, :], constant=0.0)
                pend1 = [None]
                for t in range(ntiles):
                    rows = 128 if t < 16 else 64
                    pbk = next_ab()
                    for kc in range(16):
                        MM(pb[pbk][0:rows, :], xnT[:, kc, t * 128:t * 128 + rows], Wqk[:, kc, :], kc == 0, kc == 15,
                           [f"xnT{t}", kqk], [f"pb{pbk}"])
                    if t == 1 and pend2[0] is not None:
                        pend2[0]()
                        pend2[0] = None
                    ri = t % 2
                    if t >= 2 or h == 0:
                        DMA("sp", rb[ri][0:rows, :], ropeB_d[t, 0:rows, :], [], [f"rb{ri}"])
                    zq = Ft[0]
                    ACT(zq[0:rows, :], pb[pbk][0:rows, :], AF.Identity, [f"pb{pbk}"], ["F0"])
                    zv = zq[0:rows, :].rearrange("p (a b d) -> p a b d", a=2, b=2)
                    x1 = zv[:, :, 0, :]
                    x2 = zv[:, :, 1, :]
                    rbv = rb[ri][0:rows, :].rearrange("p (a b d) -> p a b d", a=2, b=2)
                    cos = rbv[:, 0, :, :]
                    sin = rbv[:, 1, :, :]
                    tA = Ft[1][0:rows, 0:256].rearrange("p (a d) -> p a d", a=2)
                    tB = Ft[1][0:rows, 256:512].rearrange("p (a d) -> p a d", a=2)
                    tC = Ft[2][0:rows, 0:256].rearrange("p (a d) -> p a d", a=2)
                    tD = Ft[2][0:rows, 256:512].rearrange("p (a d) -> p a d", a=2)
                    rk = ["F0", f"rb{ri}"]
                    VOP("dve", "tensor_tensor", rk, ["F1"], out=tA, in0=x1, in1=cos, op=ALU.mult)
                    VOP("dve", "tensor_tensor", rk, ["F1"], out=tB, in0=x2, in1=sin, op=ALU.mult)
                    VOP("dve", "tensor_tensor", rk, ["F2"], out=tC, in0=x1, in1=sin, op=ALU.mult)
                    VOP("dve", "tensor_tensor", rk, ["F2"], out=tD, in0=x2, in1=cos, op=ALU.mult)
                    qkr = bfw[t % 2]
                    qkk = f"bfw{t % 2}"
                    qv = qkr[0:rows, :].rearrange("p (a b d) -> p a b d", a=2, b=2)
                    VOP("dve", "tensor_tensor", ["F1"], [qkk], out=qv[:, :, 0, :], in0=tA, in1=tB, op=ALU.subtract)
                    VOP("dve", "tensor_tensor", ["F2"], [qkk], out=qv[:, :, 1, :], in0=tC, in1=tD, op=ALU.add)
                    kcol = (KDC if t < 16 else 24 + KDC) + h
                    ACT(kd[0:rows, t, :], qkr[0:rows, 256:512], AF.Identity, [qkk, "small"], [f"vk{t}"], scale=small[0:rows, kcol:kcol + 1])

                    def st1b(t=t, rows=rows, qkr=qkr, qkk=qkk):
                        for j in range(4):
                            TR(pt[0:128, j * 128:j * 128 + rows], qkr[0:rows, j * 128:(j + 1) * 128], rows, [qkk, "ident"], ["pt"])
                        ACT(qkT[:, 0:4, t * 128:t * 128 + rows],
                            pt[:, 0:512].rearrange("p (a n) -> p a n", a=4)[:, :, 0:rows], AF.Identity, ["pt"], [f"qkT{t}"])
                    if pend1[0] is not None:
                        pend1[0]()
                    pend1[0] = st1b
                if h + 1 < 8:
                    nxt[0] = load_w(wb[h + 1, 0])
                    for tt in range(2):
                        DMA("sp", rb[tt][:, :], ropeB_d[tt, :, :], [], [f"rb{tt}"])
                if s == 0:
                    for b in range(2):
                        DMA("sp", StA[b], st[b, h].rearrange("(dc p) e -> p dc e", p=128), [], [f"StA{b}"])
                for t in range(ntiles):
                    rows = 128 if t < 16 else 64
                    prompt = t < 16
                    tc0 = t * 128
                    pv = next_ab()
                    for kc in range(16):
                        MM(pb[pv][0:rows, :], xnT[:, kc, tc0:tc0 + rows], Wv[:, kc, :], kc == 0, kc == 15, [f"xnT{t}", kv], [f"pb{pv}"])
                    if pend1[0] is not None:
                        pend1[0]()
                        pend1[0] = None
                    vi = t % 2
                    v = vt[vi]
                    ACT(v[0:rows, :], pb[pv][0:rows, :], AF.Identity, [f"pb{pv}"], [f"vt{vi}"])
                    pg = next_ab()
                    for kc in range(16):
                        MM(pb[pg][0:rows, :], xnT[:, kc, tc0:tc0 + rows], Wg[:, kc, :], kc == 0, kc == 15, [f"xnT{t}", kg], [f"pb{pg}"])
                    sgl = Ft[3 + vi]
                    ACT(sgl[0:rows, :], pb[pg][0:rows, :], AF.Silu, [f"pb{pg}"], [f"F{3 + vi}"])
                    for dc in range(2):
                        MM(pb[2][0:rows, 0:rows], qkT[:, 2 + dc, tc0:tc0 + rows], qkT[:, dc, tc0:tc0 + rows], dc == 0, dc == 1,
                           [f"qkT{t}"], ["pb2"])
                    scb = bfw[1]
                    if prompt:
                        VOP("dve", "tensor_tensor", ["pb2", f"DT{dti}"], ["bfw1"], out=scb[:, 0:128], in0=pb[2][:, 0:128],
                            in1=DT[dti][:, :], op=ALU.mult)
                        MM(pb[3][:, :], scb[:, 0:128], v[:, :], True, False, ["bfw1", f"vt{vi}"], ["pb3"])
                        for dc in range(2):
                            MM(pb[3][:, :], qkT[:, dc, tc0:tc0 + 128], Sb[:, dc, :], False, dc == 1, [f"qkT{t}", "Sb"], ["pb3"])
                        for dc in range(2):
                            MM(pb[4 + dc][:, :], kd[:, t, dc * 128:(dc + 1) * 128], v[:, :], True, True, [f"vk{t}", f"vt{vi}"], [f"pb{4 + dc}"])
                        for dc in range(2):
                            VOP("dve", "scalar_tensor_tensor", ["S", f"pb{4 + dc}"], ["S"], out=Sst[:, dc, :], in0=Sst[:, dc, :],
                                scalar=_CD128[h], in1=pb[4 + dc][:, :], op0=ALU.mult, op1=ALU.add)
                            ACT(Sb[:, dc, :], Sst[:, dc, :], AF.Identity, ["S"], ["Sb"])
                        if t == 15:
                            DMA("sp", ns_p[s, h].rearrange("(dc p) e -> p dc e", p=128), Sst[:, :, :], ["S"], [])
                    else:
                        VOP("dve", "tensor_tensor", ["pb2", "DTs"], ["bfw1"], out=scb[0:64, 0:64], in0=pb[2][0:64, 0:64],
                            in1=DTs[0:64, h * 64:(h + 1) * 64], op=ALU.mult)
                        MM(pb[3][0:64, :], scb[0:64, 0:64], v[0:64, :], True, False, ["bfw1", f"vt{vi}"], ["pb3"])
                        for b in range(2):
                            c0 = b * 32
                            Sx = StA[b]
                            sxk = f"StA{b}"
                            for dc in range(2):
                                ACT(Sb[:, dc, :], Sx[:, dc, :], AF.Identity, [sxk], ["Sb"])
                            qm = bfw[2 + b][:, 0:128].rearrange("p (a n) -> p a n", a=2)
                            qmk = f"bfw{2 + b}"
                            VOP("pool", "memset", [], [qmk], ap=bfw[2 + b][:, 0:128], constant=0.0)
                            VOP("pool", "tensor_copy", ["qkT16"], [qmk], out=qm[:, :, c0:c0 + 16], in_=qkT[:, 0:2, 2048 + c0:2048 + c0 + 16])
                            for dc in range(2):
                                MM(pb[3][0:64, :], qm[:, dc, :], Sb[:, dc, :], False, (b == 1 and dc == 1), [qmk, "Sb"], ["pb3"])
                            for dc in range(2):
                                MM(pb[4 + dc][:, :], kd[c0:c0 + 16, 16, dc * 128:(dc + 1) * 128], v[c0:c0 + 16, :], True, True,
                                   ["vk16", f"vt{vi}"], [f"pb{4 + dc}"])
                            for dc in range(2):
                                VOP("dve", "scalar_tensor_tensor", [sxk, f"pb{4 + dc}"], [sxk], out=Sx[:, dc, :], in0=Sx[:, dc, :],
                                    scalar=_CD16[h], in1=pb[4 + dc][:, :], op0=ALU.mult, op1=ALU.add)
                            DMA("sp", ns_s[b, h].rearrange("(dc p) e -> p dc e", p=128), Sx, [sxk], [])
                    base = 0 if prompt else 24
                    VOP("pool", "memset", [], ["ssq2"], ap=ssq[0:rows, 2:3], constant=0.0)
                    ACT(pT[0][0:rows, :], pb[3][0:rows, :], AF.Square, ["pb3"], ["pT0", "ssq2"],
                        accum=ssq[0:rows, 2:3])
                    VOP("dve", "tensor_scalar", ["ssq2"], ["rstd2"], out=rstd[0:rows, 2:3], in0=ssq[0:rows, 2:3], scalar1=1.0 / 512.0,
                        scalar2=None, op0=ALU.mult)
                    VOP("dve", "tensor_tensor", ["rstd2", "small"], ["rstd2"], out=rstd[0:rows, 2:3], in0=rstd[0:rows, 2:3],
                        in1=small[0:rows, base + EPQ + h:base + EPQ + h + 1], op=ALU.add)
                    VOP("pool", "tensor_tensor", ["rstd2", "mhalf"], ["rstd2"], out=rstd[0:rows, 2:3], in0=rstd[0:rows, 2:3],
                        in1=mhalf[0:rows, 0:1], op=ALU.pow)
                    og = pT[1 + (t % 2)]
                    ogk = f"pT{1 + (t % 2)}"
                    VOP("dve", "scalar_tensor_tensor", ["pb3", "rstd2", f"F{3 + vi}"], [ogk], out=og[0:rows, :], in0=pb[3][0:rows, :],
                        scalar=rstd[0:rows, 2:3], in1=sgl[0:rows, :], op0=ALU.mult, op1=ALU.mult)

                    def st2b(t=t, rows=rows, og=og, ogk=ogk, prompt=prompt, h=h, s=s):
                        for j in range(4):
                            TR(pt[0:128, 512 + j * 128:512 + j * 128 + rows], og[0:rows, j * 128:(j + 1) * 128], rows, [ogk, "ident"], ["pt"])
                        oi = rot("oi", 2)
                        ob = bfw[2 + oi]
                        okey = f"bfw{2 + oi}"
                        obv = ob[:, :].rearrange("p (a n) -> p a n", a=4)
                        ACT(obv[:, :, 0:rows], pt[:, 512:1024].rearrange("p (a n) -> p a n", a=4)[:, :, 0:rows], AF.Identity, ["pt"], [okey])
                        DMA("act", scr1[s, t, :, h * 4:(h + 1) * 4, 0:rows], obv[:, :, 0:rows], [okey], [f"s1_{t}"])
                    if pend2[0] is not None:
                        pend2[0]()
                    pend2[0] = st2b
                if h + 1 < 8:
                    nxt[1] = load_w(wb[h + 1, 1])
                    nxt[2] = load_w(wb[h + 1, 2])
            if pend2[0] is not None:
                pend2[0]()
                pend2[0] = None
            TRANS0(XB_KEYS)

        for s in range(2):
            ntiles = 17 if s == 0 else 16
            phase_norm(lambda t, s=s: xp[s, t * 128:(t + 1) * 128, :] if t < 16 else xs[0:64, :], gA, "gA", ntiles)
            phase_attn(s)
            phase_outproj_a(s, ntiles)
            phase_ret(s)
            phase_outproj_b(s, ntiles)

        S.assign(csems, dsems)
        _NC_CACHE["sched"] = S
        with nc.Block() as block:
            @block.tensor
            def _(e):
                S.emit("pe", e)

            @block.scalar
            def _(e):
                S.emit("act", e)

            @block.vector
            def _(e):
                S.emit("dve", e)

            @block.gpsimd
            def _(e):
                S.emit("pool", e)

            @block.sync
            def _(e):
                S.emit("sp", e, final=True)
    return nc


_NC_CACHE = {}


def _prep_shared(inp):
    f32 = np.float32
    w_in_a = np.asarray(inp["w_in_a"], f32)[0]
    wa = np.empty((16, 128, 16, 512), f32)
    for h in range(16):
        cols = np.concatenate([np.arange(h * 128, (h + 1) * 128) + off for off in (0, 2048, 4096, 6144)])
        wa[h] = w_in_a[:, cols].reshape(16, 128, 512).transpose(1, 0, 2)
    woa = np.ascontiguousarray(np.asarray(inp["w_out_a"], f32)[0].reshape(16, 128, 2048).transpose(1, 0, 2))
    w_in_b = np.asarray(inp["w_in_b"], f32)[0]
    wb = np.empty((8, 3, 128, 16, 512), f32)
    for h in range(8):
        qk = np.concatenate([np.arange(h * 256, (h + 1) * 256), 2048 + np.arange(h * 256, (h + 1) * 256)])
        blocks = [qk, 4096 + np.arange(h * 512, (h + 1) * 512), 8192 + np.arange(h * 512, (h + 1) * 512)]
        for j, cols in enumerate(blocks):
            wb[h, j] = w_in_b[:, cols].reshape(16, 128, 512).transpose(1, 0, 2)
    w_out_b = np.asarray(inp["w_out_b"], f32)[0]
    wob = np.ascontiguousarray(w_out_b.reshape(2, 16, 128, 2048).transpose(0, 2, 1, 3))
    gA = np.ascontiguousarray(np.asarray(inp["norm_a"], f32)[0].reshape(16, 128).T)
    gB = np.ascontiguousarray(np.asarray(inp["norm_b"], f32)[0].reshape(16, 128).T)
    gF = np.asarray(inp["norm_final"], f32).reshape(1, 2048)
    lam4 = np.concatenate([np.asarray(inp[k], f32)[0] for k in ("lambda_q1", "lambda_k1", "lambda_q2", "lambda_k2")]).reshape(1, 256)
    subln = np.asarray(inp["subln_a"], f32)[0].reshape(128, 1)
    shared = dict(wa=wa, woa=woa, wb=wb, wob=wob, gA=gA, gB=gB, gF=gF, lam4=lam4, subln=subln)
    shared.update(_TAB)
    return shared


def kernel(**inp):
    f32 = np.float32
    if "nc" not in _NC_CACHE:
        _NC_CACHE["nc"] = build_program()
    nc = _NC_CACHE["nc"]
    shared = _prep_shared(inp)
    x_prompt = np.asarray(inp["x_prompt"], f32)
    x_sample = np.asarray(inp["x_sample"], f32)
    cache_k = np.asarray(inp["cache_k_a"], f32)[0].reshape(16, 1024, 2048)
    cache_v = np.asarray(inp["cache_v_a"], f32)[0].reshape(16, 1024, 2048)
    state = np.asarray(inp["state_ret"], f32)[0]
    in_maps = []
    for c in range(NCORES):
        m = dict(shared)
        m["xp"] = np.ascontiguousarray(x_prompt[2 * c:2 * c + 2])
        xs = np.zeros((64, 2048), f32)
        xs[0:16] = x_sample[2 * c]
        xs[32:48] = x_sample[2 * c + 1]
        m["xs"] = xs
        m["ck"] = np.ascontiguousarray(cache_k[2 * c:2 * c + 2])
        m["cv"] = np.ascontiguousarray(cache_v[2 * c:2 * c + 2])
        m["st"] = np.ascontiguousarray(state[2 * c:2 * c + 2])
        in_maps.append(m)
    res = run_bass_kernel_spmd(nc, in_maps, core_ids=list(range(NCORES)))
    R = res.results

    def cat(name):
        return np.concatenate([np.asarray(r[name]) for r in R], axis=0)

    def cat_s(name):
        return np.concatenate([np.stack([np.asarray(r[name])[0:16], np.asarray(r[name])[32:48]]) for r in R], axis=0)

    y_prompt = cat("y_p").astype(f32)
    y_sample = cat_s("y_s").astype(f32)
    nk_p = cat("nk_p").reshape(1, 16, 2048, 32, 64).astype(f32)
    nv_p = cat("nv_p").reshape(1, 16, 2048, 16, 128).astype(f32)
    ns_p = cat("ns_p").reshape(1, 16, 8, 256, 512).astype(f32)
    nk_s = cat_s("nk_s").reshape(1, 16, 16, 32, 64).astype(f32)
    nv_s = cat_s("nv_s").reshape(1, 16, 16, 16, 128).astype(f32)
    ns_s = cat("ns_s").reshape(1, 16, 8, 256, 512).astype(f32)
    return (y_prompt, y_sample, nk_p, nv_p, ns_p, nk_s, nv_s, ns_s)
```

```python
import math
from contextlib import ExitStack

import numpy as np
import concourse.bass as bass
import concourse.mybir as mybir
from concourse.bass_utils import run_bass_kernel_spmd

F32 = mybir.dt.float32
BF = mybir.dt.bfloat16
AF = mybir.ActivationFunctionType
ALU = mybir.AluOpType

D = 2048
SEQ = 2048
NTP = 2112
NCORES = 8
NS_DMA = 8
SIG_LIMIT = 24000
ENGS = ["pe", "act", "dve", "pool", "sp"]


class Op:
    __slots__ = ("eng", "fn", "deps", "is_dma", "sig", "need_sig")

    def __init__(self, eng, fn, is_dma):
        self.eng = eng
        self.fn = fn
        self.is_dma = is_dma
        self.deps = []
        self.sig = None
        self.need_sig = False


class Sched:
    def __init__(self):
        self.q = {e: [] for e in ENGS}
        self.lastw = {}
        self.readers = {}
        self.dma_hist = {e: [] for e in ENGS}
        self.nops = 0

    def add(self, eng, fn, reads=(), writes=(), dma=False):
        op = Op(eng, fn, dma)
        deps = {}
        for k in reads:
            w = self.lastw.get(k)
            if w is not None:
                deps[w] = "raw"
            if k[0] == "p" and k[1] in "bt":
                rd = self.readers.get(k)
                if rd:
                    for e2, o2 in rd.items():
                        if e2 != "dma" and e2 != eng:
                            deps.setdefault(o2, "rar")
        for k in writes:
            w = self.lastw.get(k)
            if w is not None:
                deps.setdefault(w, "waw")
            rd = self.readers.get(k)
            if rd:
                for e2, o2 in rd.items():
                    if e2 == "dma":
                        for o3 in o2:
                            deps.setdefault(o3, "war")
                    else:
                        deps.setdefault(o2, "war")
        if dma:
            hist = self.dma_hist[eng]
            if len(hist) >= NS_DMA:
                deps[hist[len(hist) - NS_DMA]] = "raw"
            hist.append(op)
        for d, kind in deps.items():
            if d is op:
                continue
            if (not d.is_dma) and (not dma) and d.eng == eng:
                if eng == "pe" or kind != "raw":
                    continue
            op.deps.append(d)
            d.need_sig = True
        for k in reads:
            rd = self.readers.setdefault(k, {})
            if dma:
                rd.setdefault("dma", []).append(op)
            else:
                rd[eng] = op
        for k in writes:
            self.lastw[k] = op
            self.readers[k] = {}
        self.q[eng].append(op)
        self.nops += 1
        return op

    def assign(self, csems, dsems):
        self.final_dma = []
        for e in ENGS:
            cnt = 0
            epoch = 0
            dcount = [0] * NS_DMA
            di = 0
            for op in self.q[e]:
                if op.is_dma:
                    j = di % NS_DMA
                    dcount[j] += 16
                    op.sig = (dsems[e][j], dcount[j])
                    di += 1
                elif op.need_sig:
                    cnt += 1
                    if cnt > SIG_LIMIT:
                        epoch += 1
                        cnt = 1
                    op.sig = (csems[e][epoch], cnt)
            for j in range(NS_DMA):
                if dcount[j]:
                    self.final_dma.append((dsems[e][j], dcount[j]))

    def emit(self, e, engine, final=False):
        waited = {}
        for op in self.q[e]:
            for d in op.deps:
                sem, val = d.sig
                key = id(sem)
                if waited.get(key, 0) >= val:
                    continue
                engine.wait_ge(sem, val)
                waited[key] = val
            ins = op.fn(engine)
            if op.is_dma:
                ins.then_inc(op.sig[0], 16)
            elif op.need_sig:
                ins.then_inc(op.sig[0], 1)
        if final:
            for sem, val in self.final_dma:
                engine.wait_ge(sem, val)


def _tables():
    f32 = np.float32
    pos = np.zeros((17, 128), f32)
    for t in range(16):
        pos[t] = t * 128 + np.arange(128)
    for p in range(64):
        if (p % 32) < 16:
            pos[16, p] = 1024 + (p % 32)
    inv = np.power(f32(500000.0), -(np.arange(0, 16, 2, dtype=f32) / f32(16))).astype(f32)
    ang = (pos[:, :, None] * inv[None, None, :]).astype(f32).astype(np.float64)
    ca = np.cos(ang).astype(f32)
    sa = np.sin(ang).astype(f32)
    ropeAc = np.ascontiguousarray(np.broadcast_to(ca.transpose(1, 0, 2)[:, :, None, :], (128, 17, 4, 8))).reshape(128, 17 * 32)
    ropeAs = np.ascontiguousarray(np.broadcast_to(sa.transpose(1, 0, 2)[:, :, None, :], (128, 17, 4, 8))).reshape(128, 17 * 32)
    invb = (f32(1.0) / np.power(f32(10000.0), np.linspace(0.0, 1.0, 128, dtype=f32))).astype(f32)
    angb = (pos[:, :, None] * invb[None, None, :]).astype(f32).astype(np.float64)
    cb = np.cos(angb).astype(f32)
    sb = np.sin(angb).astype(f32)
    ropeB = np.zeros((17, 128, 2, 2, 128), f32)
    ropeB[:, :, 0, 0] = cb
    ropeB[:, :, 0, 1] = cb
    ropeB[:, :, 1, 0] = sb
    ropeB[:, :, 1, 1] = sb
    ropeB = ropeB.reshape(17, 128, 512)
    lg = np.log1p(-np.exp2(-5.0 - np.arange(8, dtype=f32))).astype(f32).astype(np.float64)
    idx = np.arange(128)
    q = idx[None, :]
    k = idx[:, None]
    same = (q // 64) == (k // 64)
    cross = (q // 64 == 1) & (k // 64 == 0)
    DTp = np.zeros((8, 128, 128), np.float64)
    for h in range(8):
        e = np.where(same, np.abs(q - k), np.where(cross, q - k, 0)).astype(np.float64)
        m = (same | cross).astype(np.float64)
        DTp[h] = m * np.exp(lg[h] * (e - (q + 1))) / 16.0
    DTs = np.zeros((64, 8, 64), np.float64)
    gqp = np.zeros((128, 8)); epsp = np.zeros((128, 8)); kdp = np.zeros((128, 8))
    gqs = np.ones((128, 8)); epss = np.full((128, 8), 1e-6); kds = np.zeros((128, 8))
    for h in range(8):
        gqp[:, h] = np.exp(lg[h] * (idx + 1))
        kdp[:, h] = np.exp(lg[h] * (127 - idx)) / 16.0
        for b in range(2):
            for i in range(16):
                gqs[b * 32 + i, h] = np.exp(lg[h] * (i + 1))
                kds[b * 32 + i, h] = np.exp(lg[h] * (15 - i)) / 16.0
                for j in range(16):
                    DTs[b * 32 + j, h, b * 32 + i] = np.exp(lg[h] * (abs(i - j) - (i + 1))) / 16.0
    epsp = 1e-6 / (gqp * gqp)
    epss = 1e-6 / (gqs * gqs)
    small = np.concatenate([gqp, epsp, kdp, gqs, epss, kds], axis=1).astype(f32)
    cd128 = [float(np.exp(lg[h] * 128)) for h in range(8)]
    cd16 = [float(np.exp(lg[h] * 16)) for h in range(8)]
    return dict(ropeAc=ropeAc.astype(f32), ropeAs=ropeAs.astype(f32), ropeB=ropeB,
                DTp=DTp.astype(f32), DTs=np.ascontiguousarray(DTs.reshape(64, 512)).astype(f32),
                small=small), cd128, cd16


_TAB, _CD128, _CD16 = _tables()


def build_program():
    nc = bass.Bass("TRN2", target_bir_lowering=False)

    def din(name, shape, dtype=F32):
        return nc.dram_tensor(name, list(shape), dtype, kind="ExternalInput").ap()

    def dout(name, shape, dtype=F32):
        return nc.dram_tensor(name, list(shape), dtype, kind="ExternalOutput").ap()

    def dscr(name, shape, dtype):
        return nc.dram_tensor(name, list(shape), dtype, kind="Internal").ap()

    xp = din("xp", [2, SEQ, D])
    xs = din("xs", [64, D])
    ck = din("ck", [2, 1024, D])
    cv = din("cv", [2, 1024, D])
    st = din("st", [2, 8, 256, 512])
    wa = din("wa", [16, 128, 16, 512])
    woa = din("woa", [128, 16, 2048])
    wb = din("wb", [8, 3, 128, 16, 512])
    wob = din("wob", [2, 128, 16, 2048])
    gA_d = din("gA", [128, 16])
    gB_d = din("gB", [128, 16])
    gF_d = din("gF", [1, D])
    lam_d = din("lam4", [1, 256])
    subln_d = din("subln", [128, 1])
    ropeAc_d = din("ropeAc", [128, 17 * 32])
    ropeAs_d = din("ropeAs", [128, 17 * 32])
    ropeB_d = din("ropeB", [17, 128, 512])
    DTp_d = din("DTp", [8, 128, 128])
    DTs_d = din("DTs", [64, 512])
    small_d = din("small", [128, 48])

    y_p = dout("y_p", [2, SEQ, D])
    y_s = dout("y_s", [64, D])
    nk_p = dout("nk_p", [2, SEQ, D])
    nv_p = dout("nv_p", [2, SEQ, D])
    ns_p = dout("ns_p", [2, 8, 256, 512])
    nk_s = dout("nk_s", [64, D])
    nv_s = dout("nv_s", [64, D])
    ns_s = dout("ns_s", [2, 8, 256, 512])

    scr0 = dscr("scr0", [2, 17, 128, 16, 128], BF)
    scr1 = dscr("scr1", [2, 17, 128, 32, 128], BF)
    x1s = dscr("x1s", [2, 17, 128, D], F32)

    S = Sched()

    with ExitStack() as es:
        def sb(name, shape, dtype):
            return es.enter_context(nc.sbuf_tensor("sb_" + name, list(shape), dtype))

        def pst(name, shape, dtype):
            return es.enter_context(nc.psum_tensor("ps_" + name, list(shape), dtype))

        RING = 3 * 8192
        BIG = sb("BIG", [128, RING + 16 * NTP], BF)
        wring = [BIG[:, i * 8192:(i + 1) * 8192].rearrange("p (k n) -> p k n", k=16) for i in range(3)]
        xnT = BIG[:, RING:RING + 16 * NTP].rearrange("p (k n) -> p k n", k=16)
        Wo = BIG[:, RING:RING + 16 * 2048].rearrange("p (k n) -> p k n", k=16)
        ogt = [BIG[:, 8192 * (1 + i):8192 * (1 + i) + 2048].rearrange("p (k n) -> p k n", k=16) for i in range(2)]
        gFb = BIG[:, 0:4096].bitcast(F32)
        xt = [sb(f"xt{i}", [128, D], F32) for i in range(2)]
        xnb = sb("xnb", [128, D], BF)
        Fall = sb("Fall", [128, 6 * 512], F32)
        Ft = [Fall[:, i * 512:(i + 1) * 512] for i in range(6)]
        xnb2 = Fall[:, 0:1024].bitcast(BF)
        zf = [sb(f"zf{i}", [128, 384], F32) for i in range(4)]
        qkT = sb("qkT", [128, 4, NTP], BF)
        vk = sb("vk", [128, 17 * 256], BF)
        Vh = vk[:, 0:17 * 128].rearrange("p (t e) -> p t e", t=17)
        kd = vk[:, :].rearrange("p (t e) -> p t e", t=17)
        Sst = sb("Sst", [128, 2, 512], F32)
        Sbt = sb("Sb", [128, 1024], BF)
        Sb = Sbt[:, :].rearrange("p (k n) -> p k n", k=2)
        ckT = Sbt
        pT = [sb(f"pT{i}", [128, 512], BF) for i in range(3)]
        bfw = [sb(f"bfw{i}", [128, 512], BF) for i in range(4)]
        vtt = sb("vt", [128, 1024], BF)
        vt = [vtt[:, i * 512:(i + 1) * 512] for i in range(2)]
        ckst = vtt[:, :].rearrange("p (k n) -> p k n", k=8)
        cvh = sb("cvh", [128, 8, 128], BF)
        xt0b = xt[0][:, :].bitcast(BF)
        xck = [xt0b[:, (2 * b) * 1024:(2 * b + 1) * 1024].rearrange("p (k n) -> p k n", k=8) for b in range(2)]
        xcv = [xt0b[:, (2 * b + 1) * 1024:(2 * b + 2) * 1024].rearrange("p (k n) -> p k n", k=8) for b in range(2)]
        StA = [xt[1][:, b * 1024:(b + 1) * 1024].rearrange("p (k n) -> p k n", k=2) for b in range(2)]
        ropeAc = sb("ropeAc", [128, 17 * 32], F32)
        ropeAs = sb("ropeAs", [128, 17 * 32], F32)
        rb = [sb(f"rb{i}", [128, 512], F32) for i in range(2)]
        DT = [sb(f"DT{i}", [128, 128], F32) for i in range(2)]
        DTs = sb("DTs", [64, 512], F32)
        small = sb("small", [128, 48], F32)
        gA = sb("gA", [128, 16], F32)
        gB = sb("gB", [128, 16], F32)
        lam4 = Ft[1][:, 0:256]
        lamw = sb("lamw", [128, 8], F32)
        cvec = sb("cvec", [128, 1], F32)
        ssq = sb("ssq", [128, 4], F32)
        rstd = sb("rstd", [128, 4], F32)
        rt = sb("rt", [128, 4, 32], F32)
        identf = Ft[2][:, 0:128]
        ident = sb("ident", [128, 128], BF)
        ones = sb("ones", [128, 128], BF)
        dummy = sb("dummy", [128, 4], F32)
        mhalf = sb("mhalf", [128, 1], F32)

        pb = [pst(f"pb{i}", [128, 512], F32) for i in range(7)]
        pt = pst("pt", [128, 1024], BF)
        pTa = [pT[0][:, :], pT[1][:, :], pT[2][:, :], rb[0][:, :].bitcast(BF)[:, 0:512], rb[1][:, :].bitcast(BF)[:, 0:512]]
        pTk = ["pT0", "pT1", "pT2", "rb0", "rb1"]
        sbank = [pb[0][:, :], pb[1][:, :], pb[6][:, :], pt[:, :].bitcast(F32)]
        sbk = ["pb0", "pb1", "pb6", "pt"]

        csems = {e: [es.enter_context(nc.semaphore(f"c_{e}{i}")) for i in range(4)] for e in ENGS}
        dsems = {e: [es.enter_context(nc.semaphore(f"d_{e}{i}")) for i in range(NS_DMA)] for e in ("sp", "pool", "act")}
        dsems["dve"] = dsems["sp"]
        dsems["pe"] = dsems["sp"]

        def MM(out, lhsT, rhs, start, stop, r, w):
            S.add("pe", lambda e: e.matmul(out, lhsT=lhsT, rhs=rhs, start=start, stop=stop), r, w)

        def TR(out, in_, rows, r, w):
            S.add("pe", lambda e: e.transpose(out, in_, ident[0:rows, 0:rows]), r, w)

        def ACT(out, in_, func, r, w, scale=1.0, accum=None):
            def f(e):
                kw = dict(out=out, in_=in_, func=func, scale=scale)
                if accum is not None:
                    kw["accum_out"] = accum
                return e.activation(**kw)
            S.add("act", f, r, w)

        def VOP(eng, name, r, w, **kw):
            S.add(eng, lambda e: getattr(e, name)(**kw), r, w)

        def DMA(qn, out, in_, r, w):
            S.add(qn, lambda e: e.dma_start(out=out, in_=in_), r, w, dma=True)

        state = dict(ab=0, zi=0, pi=0, wi=0, ri=0, oi=0, s3=0, s4=0, p5=0)

        def next_ab():
            state["ab"] ^= 1
            return state["ab"]

        def rot(name, n):
            v = state[name]
            state[name] = (v + 1) % n
            return v

        XNT_KEYS = [f"xnT{t}" for t in range(17)]

        VOP("pool", "memset", [], ["identf", "F2"], ap=identf, constant=0.0)
        S.add("pool", lambda e: e.affine_select(out=identf, in_=identf, pattern=[[-1, 128]],
                                                compare_op=ALU.not_equal, fill=1.0, base=0,
                                                channel_multiplier=1), ["identf"], ["identf"])
        VOP("pool", "tensor_copy", ["identf", "F2"], ["ident"], out=ident[:], in_=identf)
        VOP("pool", "memset", [], ["ones"], ap=ones[:], constant=1.0)
        VOP("pool", "memset", [], ["dummy"], ap=dummy[:], constant=0.0)
        VOP("pool", "memset", [], ["mhalf"], ap=mhalf[:], constant=-0.5)
        DMA("sp", ropeAc[:], ropeAc_d, [], ["ropeA"])
        DMA("sp", ropeAs[:], ropeAs_d, [], ["ropeA"])
        DMA("sp", DTs[:], DTs_d, [], ["DTs"])
        DMA("sp", small[:], small_d, [], ["small"])
        DMA("sp", gA[:], gA_d, [], ["gA"])
        DMA("sp", gB[:], gB_d, [], ["gB"])
        DMA("sp", lam4, lam_d.broadcast_to([128, 256]), [], ["lam4", "F1"])
        DMA("sp", cvec[:], subln_d, [], ["cvec"])
        VOP("dve", "tensor_tensor", ["lam4", "F1"], ["lamw", "F0"], out=Ft[0][:, 0:64], in0=lam4[:, 0:64], in1=lam4[:, 64:128], op=ALU.mult)
        VOP("dve", "tensor_reduce", ["lamw", "F0"], ["lamw"], out=lamw[:, 0:1], in_=Ft[0][:, 0:64], axis=mybir.AxisListType.X, op=ALU.add)
        VOP("dve", "tensor_tensor", ["lam4", "F1"], ["lamw", "F0"], out=Ft[0][:, 64:128], in0=lam4[:, 128:192], in1=lam4[:, 192:256], op=ALU.mult)
        VOP("dve", "tensor_reduce", ["lamw", "F0"], ["lamw"], out=lamw[:, 1:2], in_=Ft[0][:, 64:128], axis=mybir.AxisListType.X, op=ALU.add)
        ACT(lamw[:, 2:4], lamw[:, 0:2], AF.Exp, ["lamw"], ["lamw"])
        VOP("dve", "tensor_tensor", ["lamw"], ["lamw"], out=lamw[:, 4:5], in0=lamw[:, 3:4], in1=lamw[:, 2:3], op=ALU.subtract)
        VOP("dve", "tensor_scalar", ["lamw"], ["nlam"], out=lamw[:, 5:6], in0=lamw[:, 4:5], scalar1=-0.2, scalar2=None, op0=ALU.add)
        nlam = lamw[:, 5:6]
        VOP("dve", "tensor_scalar", ["cvec"], ["cvec"], out=cvec[:], in0=cvec[:], scalar1=float(0.8 * math.sqrt(128.0)),
            scalar2=None, op0=ALU.mult)

        def phase_norm(src_fn, gcol, gkey, ntiles, srck=None):
            pt2 = pb[6][:, :].bitcast(BF)
            for t in range(ntiles):
                rows = 128 if t < 16 else 64
                b = t % 2
                xb = xnb[:, :] if b == 0 else xnb2
                xk = ["xnb"] if b == 0 else ["F0", "F1"]
                DMA("sp", xt[b][0:rows, :], src_fn(t), ([f"{srck}_{t}"] if srck else []), [f"xt{b}"])
                VOP("pool", "memset", [], [f"ssq{b}"], ap=ssq[0:rows, b:b + 1], constant=0.0)
                ACT(xb[0:rows, :], xt[b][0:rows, :], AF.Square, [f"xt{b}"], xk + [f"ssq{b}"], accum=ssq[0:rows, b:b + 1])
                VOP("dve", "tensor_scalar", [f"ssq{b}"], [f"rstd{b}"], out=rstd[0:rows, b:b + 1], in0=ssq[0:rows, b:b + 1],
                    scalar1=1.0 / D, scalar2=1e-6, op0=ALU.mult, op1=ALU.add)
                VOP("pool", "tensor_tensor", [f"rstd{b}", "mhalf"], [f"rstd{b}"], out=rstd[0:rows, b:b + 1], in0=rstd[0:rows, b:b + 1],
                    in1=mhalf[0:rows, 0:1], op=ALU.pow)
                VOP("dve", "tensor_scalar", [f"xt{b}", f"rstd{b}"], xk, out=xb[0:rows, :], in0=xt[b][0:rows, :],
                    scalar1=rstd[0:rows, b:b + 1], scalar2=None, op0=ALU.mult)
                for half in range(2):
                    bank = pt if half == 0 else pt2
                    bkey = "pt" if half == 0 else "pb6"
                    for j in range(8):
                        kc = half * 8 + j
                        TR(bank[0:128, j * 128:j * 128 + rows], xb[0:rows, kc * 128:(kc + 1) * 128], rows, xk + ["ident"], [bkey])
                    for j in range(8):
                        kc = half * 8 + j
                        dst = xnT[:, kc, t * 128:t * 128 + rows]
                        srcp = bank[:, j * 128:j * 128 + rows]
                        if half == 0:
                            ACT(dst, srcp, AF.Identity, [bkey, gkey], [f"xnT{t}"], scale=gcol[:, kc:kc + 1])
                        else:
                            VOP("dve", "tensor_scalar", [bkey, gkey], [f"xnT{t}"], out=dst, in0=srcp,
                                scalar1=gcol[:, kc:kc + 1], scalar2=None, op0=ALU.mult)

        def load_w(src):
            slot = rot("wi", 3)
            DMA("pool", wring[slot], src, [], [f"w{slot}"])
            return wring[slot], f"w{slot}"

        def TRANS0(keys):
            VOP("pool", "memset", [], keys, ap=dummy[:, 1:2], constant=0.0)

        XA_KEYS = ["xt0", "xck0", "xck1", "xcv0", "xcv1"]
        XB_KEYS = ["xt1", "StA0", "StA1"]
        pend_fin = [None]

        def flush_fin(nstage=2):
            if pend_fin[0] is not None:
                for _ in range(nstage):
                    if pend_fin[0]:
                        pend_fin[0].pop(0)()
                if not pend_fin[0]:
                    pend_fin[0] = None

        def phase_attn(s):
            TRANS0(XA_KEYS)
            wnext = load_w(wa[0])
            for h in range(16):
                W, wkey = wnext
                if h + 1 < 16:
                    wnext = load_w(wa[h + 1])
                hc = slice(h * 128, (h + 1) * 128)

                def proj_tile(t, rows):
                    pbk = next_ab()
                    for kc in range(16):
                        MM(pb[pbk][0:rows, 0:384], xnT[:, kc, t * 128:t * 128 + rows], W[:, kc, 0:384], kc == 0, kc == 15,
                           [f"xnT{t}", wkey], [f"pb{pbk}"])
                    zi = rot("zi", 4)
                    z = zf[zi]
                    ACT(z[0:rows, :], pb[pbk][0:rows, 0:384], AF.Identity, [f"pb{pbk}"], [f"zf{zi}"])
                    ACT(Vh[0:rows, t, :], pb[pbk][0:rows, 256:384], AF.Identity, [f"pb{pbk}"], [f"vk{t}"])
                    zv = z[0:rows, 0:256].rearrange("p (s d) -> p s d", d=64)
                    x1 = zv[:, :, 0:8]
                    x2 = zv[:, :, 8:16]
                    cos = ropeAc[0:rows, t * 32:(t + 1) * 32].rearrange("p (s d) -> p s d", d=8)
                    sin = ropeAs[0:rows, t * 32:(t + 1) * 32].rearrange("p (s d) -> p s d", d=8)
                    rv = [rt[0:rows, i, :].rearrange("p (s d) -> p s d", d=8) for i in range(4)]
                    rk = [f"zf{zi}", "ropeA"]
                    VOP("dve", "tensor_tensor", rk, ["rt"], out=rv[0], in0=x1, in1=cos, op=ALU.mult)
                    VOP("dve", "tensor_tensor", rk, ["rt"], out=rv[1], in0=x2, in1=sin, op=ALU.mult)
                    VOP("dve", "tensor_tensor", rk, ["rt"], out=rv[2], in0=x1, in1=sin, op=ALU.mult)
                    VOP("dve", "tensor_tensor", rk, ["rt"], out=rv[3], in0=x2, in1=cos, op=ALU.mult)
                    VOP("dve", "tensor_tensor", ["rt"], [f"zf{zi}"], out=x1, in0=rv[0], in1=rv[1], op=ALU.subtract)
                    VOP("dve", "tensor_tensor", ["rt"], [f"zf{zi}"], out=x2, in0=rv[2], in1=rv[3], op=ALU.add)
                    qb = bfw[2 + (t % 2)]
                    qkey = f"bfw{2 + (t % 2)}"
                    VOP("dve", "tensor_copy", [f"zf{zi}"], [qkey], out=qb[0:rows, 0:256], in_=z[0:rows, 0:256])
                    if t < 16:
                        DMA("sp", nk_p[s, t * 128:(t + 1) * 128, hc], z[:, 128:256], [f"zf{zi}"], [])
                        DMA("sp", nv_p[s, t * 128:(t + 1) * 128, hc], z[:, 256:384], [f"zf{zi}"], [])
                    else:
                        DMA("sp", nk_s[0:64, hc], z[0:64, 128:256], [f"zf{zi}"], [])
                        DMA("sp", nv_s[0:64, hc], z[0:64, 256:384], [f"zf{zi}"], [])

                def proj_tile2(t, rows):
                    qb = bfw[2 + (t % 2)]
                    qkey = f"bfw{2 + (t % 2)}"
                    TR(pt[0:128, 0:rows], qb[0:rows, 0:128], rows, [qkey, "ident"], ["pt"])
                    TR(pt[0:128, 128:128 + rows], qb[0:rows, 128:256], rows, [qkey, "ident"], ["pt"])
                    ACT(qkT[:, 0:2, t * 128:t * 128 + rows],
                        pt[:, 0:256].rearrange("p (a n) -> p a n", a=2)[:, :, 0:rows], AF.Identity, ["pt"], [f"qkT{t}"])

                def gate(col0, n, tiles):
                    pbk = next_ab()
                    for kc in range(16):
                        MM(pb[pbk][:, 0:n], W[:, kc, 384:512], xnT[:, kc, col0:col0 + n], kc == 0, kc == 15,
                           [f"xnT{t}" for t in tiles] + [wkey], [f"pb{pbk}"])
                    ACT(Ft[3][:, 0:n], pb[pbk][:, 0:n], AF.Exp, [f"pb{pbk}"], ["F3"], scale=-1.0)
                    ACT(Ft[5][:, 0:n], pb[pbk][:, 0:n], AF.Identity, [f"pb{pbk}"], ["F5"])
                    VOP("dve", "tensor_scalar", ["F3"], ["F3"], out=Ft[3][:, 0:n], in0=Ft[3][:, 0:n], scalar1=1.0, scalar2=None, op0=ALU.add)
                    ACT(Ft[3][:, 0:n], Ft[3][:, 0:n], AF.Ln, ["F3"], ["F3"])
                    ACT(Ft[3][:, 0:n], Ft[3][:, 0:n], AF.Exp, ["F3"], ["F3"], scale=-1.0)
                    VOP("dve", "tensor_tensor", ["F3", "F5"], ["F4"], out=Ft[4][:, 0:n], in0=Ft[3][:, 0:n],
                        in1=Ft[5][:, 0:n], op=ALU.mult)

                def finish(n, dst_ap, skeys):
                    flush_fin()
                    finish_a(n, dst_ap, skeys)
                    pend_fin[0] = [lambda: finish_b(n, dst_ap, skeys)]

                def finish_a(n, dst_ap, skeys):
                    f0, f1, f2 = Ft[0], Ft[1], Ft[2]
                    ACT(f0[:, 0:n], pb[3][:, 0:n], AF.Ln, ["pb3"], ["F0"])
                    ACT(f0[:, 0:n], f0[:, 0:n], AF.Exp, ["F0"], ["F0"], scale=-1.0)
                    VOP("dve", "tensor_tensor", ["pb2", "F0"], ["F1"], out=f1[:, 0:n], in0=pb[2][:, 0:n], in1=f0[:, 0:n], op=ALU.mult)
                    ACT(f2[:, 0:n], pb[5][:, 0:n], AF.Ln, ["pb5"], ["F2"])
                    ACT(f2[:, 0:n], f2[:, 0:n], AF.Exp, ["F2"], ["F2"], scale=-1.0)
                    VOP("dve", "tensor_tensor", ["pb4", "F2"], ["F2"], out=f2[:, 0:n], in0=pb[4][:, 0:n], in1=f2[:, 0:n], op=ALU.mult)
                    VOP("dve", "scalar_tensor_tensor", ["F1", "F2", "nlam"], ["F1"], out=f1[:, 0:n], in0=f2[:, 0:n], scalar=nlam,
                        in1=f1[:, 0:n], op0=ALU.mult, op1=ALU.add)
                    ACT(bfw[0][:, 0:n], f1[:, 0:n], AF.Square, ["F1"], ["bfw0"])

                def finish_b(n, dst_ap, skeys):
                    f0, f1, f2 = Ft[0], Ft[1], Ft[2]
                    MM(pb[6][:, 0:n], ones[:, :], bfw[0][:, 0:n], True, True, ["bfw0", "ones"], ["pb6"])
                    VOP("dve", "tensor_scalar", ["pb6"], ["F0"], out=f0[:, 0:n], in0=pb[6][:, 0:n], scalar1=128.0 * 1e-5, scalar2=None,
                        op0=ALU.add)
                    ACT(f0[:, 0:n], f0[:, 0:n], AF.Ln, ["F0"], ["F0"])
                    ACT(f0[:, 0:n], f0[:, 0:n], AF.Exp, ["F0"], ["F0"], scale=-0.5)
                    VOP("dve", "tensor_tensor", ["F1", "F0"], ["F1"], out=f1[:, 0:n], in0=f1[:, 0:n], in1=f0[:, 0:n], op=ALU.mult)
                    VOP("dve", "scalar_tensor_tensor", ["F1", "F4", "cvec"], ["bfw1"], out=bfw[1][:, 0:n], in0=f1[:, 0:n],
                        scalar=cvec[:, 0:1], in1=Ft[4][:, 0:n], op0=ALU.mult, op1=ALU.mult)
                    if n != 512:
                        VOP("pool", "memset", [], ["bfw1"], ap=bfw[1][:, 16:32], constant=0.0)
                        VOP("pool", "memset", [], ["bfw1"], ap=bfw[1][:, 48:64], constant=0.0)
                        DMA("sp", dst_ap, bfw[1][:, 0:64], ["bfw1"], skeys)
                    else:
                        DMA("sp", dst_ap, bfw[1][:, :].rearrange("p (t n) -> p t n", t=4), ["bfw1"], skeys)

                def attn_group(G):
                    nblk = 4 * G + 4

                    def st_pair(i):
                        infos = []
                        qlo = max(i, 4 * G) * 128
                        qhi = (4 * G + 4) * 128
                        n = qhi - qlo
                        co = qlo - 4 * G * 128
                        banks = []
                        for sub in range(2):
                            r0 = sub * 64
                            bi = rot("s4", 4)
                            banks.append(bi)
                            MM(sbank[bi][:, 0:n], qkT[r0:r0 + 64, 1, i * 128:(i + 1) * 128], qkT[r0:r0 + 64, 0, qlo:qhi], True, True,
                               [f"qkT{i}"] + [f"qkT{tt}" for tt in range(qlo // 128, 4 * G + 4)], [sbk[bi]])
                        for sub in range(2):
                            bi = banks[sub]
                            pi = rot("p5", 5)
                            p = pTa[pi]
                            ACT(p[:, 0:n], sbank[bi][:, 0:n], AF.Exp, [sbk[bi]], [pTk[pi]], scale=0.125)
                            if i >= 4 * G:
                                VOP("pool", "memset", [], [pTk[pi]], ap=p[64:128, 0:64], constant=0.0)
                            infos.append((sub, i, n, co, pi))
                        return infos

                    def pv_pair(infos):
                        for (sub, i, n, co, pi) in infos:
                            p = pTa[pi]
                            O = pb[2 + 2 * sub]
                            Dn = pb[3 + 2 * sub]
                            MM(O[:, co:512], Vh[:, i, :], p[:, 0:n], i == 0, i == nblk - 1, [f"vk{i}", pTk[pi]], [f"pb{2 + 2 * sub}"])
                            MM(Dn[:, co:512], ones[:, :], p[:, 0:n], i == 0, i == nblk - 1, ["ones", pTk[pi]], [f"pb{3 + 2 * sub}"])

                    pend = []
                    for i in range(nblk):
                        pend.append(st_pair(i))
                        if len(pend) > 1:
                            pv_pair(pend.pop(0))
                    while pend:
                        pv_pair(pend.pop(0))
                    finish(512, scr0[s, 4 * G:4 * G + 4, :, h, :].rearrange("t p n -> p t n"), [f"s0_{tt}" for tt in range(4 * G, 4 * G + 4)])

                def sample_prefetch():
                    for b in range(2):
                        DMA("pool", xck[b], ck[b].rearrange("(kb p) c -> p kb c", p=128)[:, :, hc], [], [f"xck{b}"])
                        DMA("pool", xcv[b], cv[b].rearrange("(kb p) c -> p kb c", p=128)[:, :, hc], [], [f"xcv{b}"])

                def sample_attn():
                    for b in range(2):
                        ckst = xck[b]
                        cvh = xcv[b]
                        for kb in range(8):
                            TR(pt[:, kb * 128:(kb + 1) * 128], ckst[:, kb, :], 128, [f"xck{b}", "ident"], ["pt"])
                        ACT(ckT[:, :], pt[:, :], AF.Identity, ["pt"], ["Sb"])
                        c0 = b * 32
                        for sub in range(2):
                            r0 = sub * 64
                            O = pb[2 + 2 * sub]
                            Dn = pb[3 + 2 * sub]
                            okey = f"pb{2 + 2 * sub}"
                            dkey = f"pb{3 + 2 * sub}"
                            pbk = next_ab()
                            qs = qkT[r0:r0 + 64, 0, 2048 + c0:2048 + c0 + 16]
                            for kb in range(8):
                                MM(pb[pbk][:, kb * 16:(kb + 1) * 16], ckT[r0:r0 + 64, kb * 128:(kb + 1) * 128], qs, True, True,
                                   ["Sb", "qkT16"], [f"pb{pbk}"])
                            MM(pb[pbk][0:64, 128:192], qkT[r0:r0 + 64, 1, 2048:2112], qkT[r0:r0 + 64, 0, 2048:2112], True, True,
                               ["qkT16"], [f"pb{pbk}"])
                            pi = rot("pi", 3)
                            p = pT[pi]
                            ACT(p[:, 0:128], pb[pbk][:, 0:128], AF.Exp, [f"pb{pbk}"], [f"pT{pi}"], scale=0.125)
                            ACT(p[0:64, 128:192], pb[pbk][0:64, 128:192], AF.Exp, [f"pb{pbk}"], [f"pT{pi}"], scale=0.125)
                            for kb in range(8):
                                MM(O[:, c0:c0 + 16], cvh[:, kb, :], p[:, kb * 16:(kb + 1) * 16], kb == 0, False, [f"xcv{b}", f"pT{pi}"], [okey])
                            MM(O[:, c0:c0 + 16], Vh[c0:c0 + 16, 16, :], p[c0:c0 + 16, 128 + c0:128 + c0 + 16], False, True,
                               ["vk16", f"pT{pi}"], [okey])
                            for kb in range(8):
                                MM(Dn[:, c0:c0 + 16], ones[:, :], p[:, kb * 16:(kb + 1) * 16], kb == 0, False, ["ones", f"pT{pi}"], [dkey])
                            MM(Dn[:, c0:c0 + 16], ones[c0:c0 + 16, :], p[c0:c0 + 16, 128 + c0:128 + c0 + 16], False, True,
                               ["ones", f"pT{pi}"], [dkey])
                    finish(48, scr0[s, 16, :, h, 0:64], ["s0_16"])

                if s == 0:
                    sample_prefetch()
                for G in range(4):
                    t0 = 4 * G
                    proj_tile(t0, 128)
                    proj_tile(t0 + 1, 128)
                    flush_fin(1)
                    proj_tile2(t0, 128)
                    proj_tile(t0 + 2, 128)
                    flush_fin(1)
                    proj_tile2(t0 + 1, 128)
                    proj_tile(t0 + 3, 128)
                    proj_tile2(t0 + 2, 128)
                    gate(G * 512, 512, range(4 * G, 4 * G + 4))
                    proj_tile2(t0 + 3, 128)
                    attn_group(G)
                if s == 0:
                    proj_tile(16, 64)
                    flush_fin()
                    gate(2048, 64, [16])
                    proj_tile2(16, 64)
                    sample_attn()

            flush_fin()
            TRANS0(XA_KEYS)

        QK_KEYS = [f"qkT{t}" for t in range(17)]
        VK_KEYS = [f"vk{t}" for t in range(17)]
        qkflat = qkT[:, :, :].rearrange("p a n -> p (a n)")

        def TRANS(keys):
            VOP("pool", "memset", [], keys, ap=dummy[:, 0:1], constant=0.0)

        def phase_outproj_a(s, ntiles):
            WaL = BIG[:, 0:12 * 2048].rearrange("p (k n) -> p k n", k=12)
            WaH = qkflat[:, 0:4 * 2048].rearrange("p (k n) -> p k n", k=4)
            oga = [vk[:, b * 2048:(b + 1) * 2048].rearrange("p (k n) -> p k n", k=16) for b in range(2)]
            pt2 = pb[6][:, :].bitcast(BF)
            TRANS(["w0", "w1", "w2"] + QK_KEYS + VK_KEYS + ["WaL0", "WaL1", "WaL2", "WaL3", "WaH0", "WaH1", "WaH2", "WaH3", "oga0", "oga1"])
            for n in range(4):
                cs = slice(n * 512, (n + 1) * 512)
                DMA("pool", WaL[:, :, cs], woa[:, 0:12, cs], [], [f"WaL{n}"])
                DMA("pool", WaH[:, :, cs], woa[:, 12:16, cs], [], [f"WaH{n}"])
            pendB = []
            nxt_loads = []

            def loads_a(t):
                rows = 128 if t < 16 else 64
                b = t % 2
                xsrc = xp[s, t * 128:(t + 1) * 128, :] if t < 16 else xs[0:64, :]
                if t < 16:
                    DMA("sp", oga[b], scr0[s, t], [f"s0_{t}"], [f"oga{b}"])
                else:
                    DMA("sp", oga[b][:, :, 0:64], scr0[s, t, :, :, 0:64], [f"s0_{t}"], [f"oga{b}"])
                DMA("sp", xt[b][0:rows, :], xsrc, [], [f"xt{b}"])
            loads_a(0)
            loads_a(1)
            for t in range(ntiles):
                rows = 128 if t < 16 else 64
                b = t % 2
                for n in range(4):
                    cs = slice(n * 512, (n + 1) * 512)
                    for kc in range(16):
                        rhs = WaL[:, kc, cs] if kc < 12 else WaH[:, kc - 12, cs]
                        MM(pb[n][0:rows, :], oga[b][:, kc, 0:rows], rhs, kc == 0, kc == 15,
                           [f"oga{b}", f"WaL{n}", f"WaH{n}"], [f"pb{n}"])
                    VOP("dve", "tensor_tensor", [f"pb{n}", f"xt{b}"], [f"xt{b}"], out=xt[b][0:rows, cs],
                        in0=pb[n][0:rows, :], in1=xt[b][0:rows, cs], op=ALU.add)
                DMA("sp", x1s[s, t, 0:rows, :], xt[b][0:rows, :], [f"xt{b}"], [f"x1_{t}"])
                while pendB:
                    pendB.pop(0)()
                nxt_loads.append(t + 2)
                xb = xnb[:, :] if b == 0 else xnb2
                xk = ["xnb"] if b == 0 else ["F0", "F1"]
                VOP("pool", "memset", [], [f"ssq{b}"], ap=ssq[0:rows, b:b + 1], constant=0.0)
                ACT(xb[0:rows, :], xt[b][0:rows, :], AF.Square, [f"xt{b}"], xk + [f"ssq{b}"], accum=ssq[0:rows, b:b + 1])
                VOP("dve", "tensor_scalar", [f"ssq{b}"], [f"rstd{b}"], out=rstd[0:rows, b:b + 1], in0=ssq[0:rows, b:b + 1],
                    scalar1=1.0 / D, scalar2=1e-6, op0=ALU.mult, op1=ALU.add)
                VOP("pool", "tensor_tensor", [f"rstd{b}", "mhalf"], [f"rstd{b}"], out=rstd[0:rows, b:b + 1], in0=rstd[0:rows, b:b + 1],
                    in1=mhalf[0:rows, 0:1], op=ALU.pow)
                VOP("dve", "tensor_scalar", [f"xt{b}", f"rstd{b}"], xk, out=xb[0:rows, :], in0=xt[b][0:rows, :],
                    scalar1=rstd[0:rows, b:b + 1], scalar2=None, op0=ALU.mult)
                def partB(t=t, rows=rows, xb=xb, xk=xk):
                    for half in range(2):
                        bank = pt if half == 0 else pt2
                        bkey = "pt" if half == 0 else "pb6"
                        for j in range(8):
                            kc = half * 8 + j
                            TR(bank[0:128, j * 128:j * 128 + rows], xb[0:rows, kc * 128:(kc + 1) * 128], rows, xk + ["ident"], [bkey])
                        for j in range(8):
                            kc = half * 8 + j
                            dst = xnT[:, kc, t * 128:t * 128 + rows]
                            srcp = bank[:, j * 128:j * 128 + rows]
                            if half == 0:
                                ACT(dst, srcp, AF.Identity, [bkey, "gB"], [f"xnT{t}"], scale=gB[:, kc:kc + 1])
                            else:
                                VOP("dve", "tensor_scalar", [bkey, "gB"], [f"xnT{t}"], out=dst, in0=srcp,
                                    scalar1=gB[:, kc:kc + 1], scalar2=None, op0=ALU.mult)
                pendB.append(partB)
                while nxt_loads:
                    tn = nxt_loads.pop(0)
                    if tn < ntiles:
                        loads_a(tn)
            while pendB:
                pendB.pop(0)()
            TRANS(["w0", "w1", "w2"] + QK_KEYS + VK_KEYS + ["WaL0", "WaL1", "WaL2", "WaL3", "WaH0", "WaH1", "WaH2", "WaH3", "oga0", "oga1"])

        def phase_outproj_b(s, ntiles):
            WbL = BIG[:, 0:28 * 2048].rearrange("p (k n) -> p k n", k=28)
            WbH = qkflat[:, 0:4 * 2048].rearrange("p (k n) -> p k n", k=4)
            ogb = [vk[:, 0:4096].rearrange("p (k n) -> p k n", k=32),
                   Fall[:, 0:2048].bitcast(BF).rearrange("p (k n) -> p k n", k=32)]
            ogk = [["ogb0"], ["F0", "F1", "F2", "F3"]]
            gFlo = xnb[:, :].bitcast(F32)
            gFhi = Sst[:, :, :].rearrange("p a n -> p (a n)")
            allk = (["w0", "w1", "w2"] + XNT_KEYS + QK_KEYS + VK_KEYS + ["xnb", "S", "F0", "F1", "F2", "F3", "ogb0", "gF"]
                    + [f"WbL{n}" for n in range(4)] + [f"WbH{n}" for n in range(4)])
            TRANS(allk)
            for n in range(4):
                cs = slice(n * 512, (n + 1) * 512)
                DMA("pool", WbL[:, 0:16, cs], wob[0][:, :, cs], [], [f"WbL{n}"])
                DMA("pool", WbL[:, 16:28, cs], wob[1][:, 0:12, cs], [], [f"WbL{n}"])
                DMA("pool", WbH[:, :, cs], wob[1][:, 12:16, cs], [], [f"WbH{n}"])
            DMA("sp", gFlo, gF_d[:, 0:1024].broadcast_to([128, 1024]), [], ["gF"])
            DMA("sp", gFhi, gF_d[:, 1024:2048].broadcast_to([128, 1024]), [], ["gF"])
            def loads_b(t):
                rows = 128 if t < 16 else 64
                b = t % 2
                if t < 16:
                    DMA("sp", ogb[b], scr1[s, t], [f"s1_{t}"], ogk[b])
                else:
                    DMA("sp", ogb[b][:, :, 0:64], scr1[s, t, :, :, 0:64], [f"s1_{t}"], ogk[b])
                DMA("sp", xt[b][0:rows, :], x1s[s, t, 0:rows, :], [f"x1_{t}"], [f"xt{b}"])
            loads_b(0)
            loads_b(1)
            for t in range(ntiles):
                rows = 128 if t < 16 else 64
                b = t % 2
                for n in range(4):
                    cs = slice(n * 512, (n + 1) * 512)
                    for kc in range(32):
                        rhs = WbL[:, kc, cs] if kc < 28 else WbH[:, kc - 28, cs]
                        MM(pb[n][0:rows, :], ogb[b][:, kc, 0:rows], rhs, kc == 0, kc == 31,
                           ogk[b] + [f"WbL{n}", f"WbH{n}"], [f"pb{n}"])
                    VOP("dve", "tensor_tensor", [f"pb{n}", f"xt{b}"], [f"xt{b}"], out=xt[b][0:rows, cs],
                        in0=pb[n][0:rows, :], in1=xt[b][0:rows, cs], op=ALU.add)
                    VOP("pool", "memset", [], [f"ssq{n}"], ap=ssq[0:rows, n:n + 1], constant=0.0)
                    ACT(bfw[n][0:rows, :], xt[b][0:rows, cs], AF.Square, [f"xt{b}"], [f"bfw{n}", f"ssq{n}"], accum=ssq[0:rows, n:n + 1])
                VOP("dve", "tensor_reduce", [f"ssq{n}" for n in range(4)], ["rstdf"], out=rstd[0:rows, 3:4], in_=ssq[0:rows, 0:4],
                    axis=mybir.AxisListType.X, op=ALU.add)
                VOP("dve", "tensor_scalar", ["rstdf"], ["rstdf"], out=rstd[0:rows, 3:4], in0=rstd[0:rows, 3:4],
                    scalar1=1.0 / D, scalar2=1e-6, op0=ALU.mult, op1=ALU.add)
                VOP("pool", "tensor_tensor", ["rstdf", "mhalf"], ["rstdf"], out=rstd[0:rows, 3:4], in0=rstd[0:rows, 3:4],
                    in1=mhalf[0:rows, 0:1], op=ALU.pow)
                VOP("dve", "scalar_tensor_tensor", [f"xt{b}", "rstdf", "gF"], [f"xt{b}"], out=xt[b][0:rows, 0:1024],
                    in0=xt[b][0:rows, 0:1024], scalar=rstd[0:rows, 3:4], in1=gFlo[0:rows, :], op0=ALU.mult, op1=ALU.mult)
                VOP("dve", "scalar_tensor_tensor", [f"xt{b}", "rstdf", "gF"], [f"xt{b}"], out=xt[b][0:rows, 1024:2048],
                    in0=xt[b][0:rows, 1024:2048], scalar=rstd[0:rows, 3:4], in1=gFhi[0:rows, :], op0=ALU.mult, op1=ALU.mult)
                dst = y_p[s, t * 128:(t + 1) * 128, :] if t < 16 else y_s[0:64, :]
                DMA("sp", dst, xt[b][0:rows, :], [f"xt{b}"], [])
                if t + 2 < ntiles:
                    loads_b(t + 2)
            TRANS(allk)

        def phase_ret(s):
            ntiles = 17 if s == 0 else 16
            TRANS0(XB_KEYS)
            GQ, EPQ, KDC = 0, 8, 16
            pend2 = [None]
            nxt = [load_w(wb[0, 0]), load_w(wb[0, 1]), load_w(wb[0, 2])]
            for h in range(8):
                (Wqk, kqk), (Wv, kv), (Wg, kg) = nxt
                nxt = [None, None, None]
                dti = h % 2
                DMA("sp", DT[dti][:, :], DTp_d[h], [], [f"DT{dti}"])

                VOP("pool", "memset", [], ["S"], ap=Sst[:, :, :], constant=0.0)
                VOP("pool", "memset", [], ["Sb"], ap=Sbt[:, :], constant=0.0)
                pend1 = [None]
                for t in range(ntiles):
                    rows = 128 if t < 16 else 64
                    pbk = next_ab()
                    for kc in range(16):
                        MM(pb[pbk][0:rows, :], xnT[:, kc, t * 128:t * 128 + rows], Wqk[:, kc, :], kc == 0, kc == 15,
                           [f"xnT{t}", kqk], [f"pb{pbk}"])
                    if t == 1 and pend2[0] is not None:
                        pend2[0]()
                        pend2[0] = None
                    ri = t % 2
                    if t >= 2 or h == 0:
                        DMA("sp", rb[ri][0:rows, :], ropeB_d[t, 0:rows, :], [], [f"rb{ri}"])
                    zq = Ft[0]
                    ACT(zq[0:rows, :], pb[pbk][0:rows, :], AF.Identity, [f"pb{pbk}"], ["F0"])
                    zv = zq[0:rows, :].rearrange("p (a b d) -> p a b d", a=2, b=2)
                    x1 = zv[:, :, 0, :]
                    x2 = zv[:, :, 1, :]
                    rbv = rb[ri][0:rows, :].rearrange("p (a b d) -> p a b d", a=2, b=2)
                    cos = rbv[:, 0, :, :]
                    sin = rbv[:, 1, :, :]
                    tA = Ft[1][0:rows, 0:256].rearrange("p (a d) -> p a d", a=2)
                    tB = Ft[1][0:rows, 256:512].rearrange("p (a d) -> p a d", a=2)
                    tC = Ft[2][0:rows, 0:256].rearrange("p (a d) -> p a d", a=2)
                    tD = Ft[2][0:rows, 256:512].rearrange("p (a d) -> p a d", a=2)
                    rk = ["F0", f"rb{ri}"]
                    VOP("dve", "tensor_tensor", rk, ["F1"], out=tA, in0=x1, in1=cos, op=ALU.mult)
                    VOP("dve", "tensor_tensor", rk, ["F1"], out=tB, in0=x2, in1=sin, op=ALU.mult)
                    VOP("dve", "tensor_tensor", rk, ["F2"], out=tC, in0=x1, in1=sin, op=ALU.mult)
                    VOP("dve", "tensor_tensor", rk, ["F2"], out=tD, in0=x2, in1=cos, op=ALU.mult)
                    qkr = bfw[t % 2]
                    qkk = f"bfw{t % 2}"
                    qv = qkr[0:rows, :].rearrange("p (a b d) -> p a b d", a=2, b=2)
                    VOP("dve", "tensor_tensor", ["F1"], [qkk], out=qv[:, :, 0, :], in0=tA, in1=tB, op=ALU.subtract)
                    VOP("dve", "tensor_tensor", ["F2"], [qkk], out=qv[:, :, 1, :], in0=tC, in1=tD, op=ALU.add)
                    kcol = (KDC if t < 16 else 24 + KDC) + h
                    ACT(kd[0:rows, t, :], qkr[0:rows, 256:512], AF.Identity, [qkk, "small"], [f"vk{t}"], scale=small[0:rows, kcol:kcol + 1])

                    def st1b(t=t, rows=rows, qkr=qkr, qkk=qkk):
                        for j in range(4):
                            TR(pt[0:128, j * 128:j * 128 + rows], qkr[0:rows, j * 128:(j + 1) * 128], rows, [qkk, "ident"], ["pt"])
                        ACT(qkT[:, 0:4, t * 128:t * 128 + rows],
                            pt[:, 0:512].rearrange("p (a n) -> p a n", a=4)[:, :, 0:rows], AF.Identity, ["pt"], [f"qkT{t}"])
                    if pend1[0] is not None:
                        pend1[0]()
                    pend1[0] = st1b
                if h + 1 < 8:
                    nxt[0] = load_w(wb[h + 1, 0])
                    for tt in range(2):
                        DMA("sp", rb[tt][:, :], ropeB_d[tt, :, :], [], [f"rb{tt}"])
                if s == 0:
                    for b in range(2):
                        DMA("sp", StA[b], st[b, h].rearrange("(dc p) e -> p dc e", p=128), [], [f"StA{b}"])
                for t in range(ntiles):
                    rows = 128 if t < 16 else 64
                    prompt = t < 16
                    tc0 = t * 128
                    pv = next_ab()
                    for kc in range(16):
                        MM(pb[pv][0:rows, :], xnT[:, kc, tc0:tc0 + rows], Wv[:, kc, :], kc == 0, kc == 15, [f"xnT{t}", kv], [f"pb{pv}"])
                    if pend1[0] is not None:
                        pend1[0]()
                        pend1[0] = None
                    vi = t % 2
                    v = vt[vi]
                    ACT(v[0:rows, :], pb[pv][0:rows, :], AF.Identity, [f"pb{pv}"], [f"vt{vi}"])
                    pg = next_ab()
                    for kc in range(16):
                        MM(pb[pg][0:rows, :], xnT[:, kc, tc0:tc0 + rows], Wg[:, kc, :], kc == 0, kc == 15, [f"xnT{t}", kg], [f"pb{pg}"])
                    sgl = Ft[3 + vi]
                    ACT(sgl[0:rows, :], pb[pg][0:rows, :], AF.Silu, [f"pb{pg}"], [f"F{3 + vi}"])
                    for dc in range(2):
                        MM(pb[2][0:rows, 0:rows], qkT[:, 2 + dc, tc0:tc0 + rows], qkT[:, dc, tc0:tc0 + rows], dc == 0, dc == 1,
                           [f"qkT{t}"], ["pb2"])
                    scb = bfw[1]
                    if prompt:
                        VOP("dve", "tensor_tensor", ["pb2", f"DT{dti}"], ["bfw1"], out=scb[:, 0:128], in0=pb[2][:, 0:128],
                            in1=DT[dti][:, :], op=ALU.mult)
                        MM(pb[3][:, :], scb[:, 0:128], v[:, :], True, False, ["bfw1", f"vt{vi}"], ["pb3"])
                        for dc in range(2):
                            MM(pb[3][:, :], qkT[:, dc, tc0:tc0 + 128], Sb[:, dc, :], False, dc == 1, [f"qkT{t}", "Sb"], ["pb3"])
                        for dc in range(2):
                            MM(pb[4 + dc][:, :], kd[:, t, dc * 128:(dc + 1) * 128], v[:, :], True, True, [f"vk{t}", f"vt{vi}"], [f"pb{4 + dc}"])
                        for dc in range(2):
                            VOP("dve", "scalar_tensor_tensor", ["S", f"pb{4 + dc}"], ["S"], out=Sst[:, dc, :], in0=Sst[:, dc, :],
                                scalar=_CD128[h], in1=pb[4 + dc][:, :], op0=ALU.mult, op1=ALU.add)
                            ACT(Sb[:, dc, :], Sst[:, dc, :], AF.Identity, ["S"], ["Sb"])
                        if t == 15:
                            DMA("sp", ns_p[s, h].rearrange("(dc p) e -> p dc e", p=128), Sst[:, :, :], ["S"], [])
                    else:
                        VOP("dve", "tensor_tensor", ["pb2", "DTs"], ["bfw1"], out=scb[0:64, 0:64], in0=pb[2][0:64, 0:64],
                            in1=DTs[0:64, h * 64:(h + 1) * 64], op=ALU.mult)
                        MM(pb[3][0:64, :], scb[0:64, 0:64], v[0:64, :], True, False, ["bfw1", f"vt{vi}"], ["pb3"])
                        for b in range(2):
                            c0 = b * 32
                            Sx = StA[b]
                            sxk = f"StA{b}"
                            for dc in range(2):
                                ACT(Sb[:, dc, :], Sx[:, dc, :], AF.Identity, [sxk], ["Sb"])
                            qm = bfw[2 + b][:, 0:128].rearrange("p (a n) -> p a n", a=2)
                            qmk = f"bfw{2 + b}"
                            VOP("pool", "memset", [], [qmk], ap=bfw[2 + b][:, 0:128], constant=0.0)
                            VOP("pool", "tensor_copy", ["qkT16"], [qmk], out=qm[:, :, c0:c0 + 16], in_=qkT[:, 0:2, 2048 + c0:2048 + c0 + 16])
                            for dc in range(2):
                                MM(pb[3][0:64, :], qm[:, dc, :], Sb[:, dc, :], False, (b == 1 and dc == 1), [qmk, "Sb"], ["pb3"])
                            for dc in range(2):
                                MM(pb[4 + dc][:, :], kd[c0:c0 + 16, 16, dc * 128:(dc + 1) * 128], v[c0:c0 + 16, :], True, True,
                                   ["vk16", f"vt{vi}"], [f"pb{4 + dc}"])
                            for dc in range(2):
                                VOP("dve", "scalar_tensor_tensor", [sxk, f"pb{4 + dc}"], [sxk], out=Sx[:, dc, :], in0=Sx[:, dc, :],
                                    scalar=_CD16[h], in1=pb[4 + dc][:, :], op0=ALU.mult, op1=ALU.add)
                            DMA("sp", ns_s[b, h].rearrange("(dc p) e -> p dc e", p=128), Sx, [sxk], [])
                    base = 0 if prompt else 24
                    VOP("pool", "memset", [], ["ssq2"], ap=ssq[0:rows, 2:3], constant=0.0)
                    ACT(pT[0][0:rows, :], pb[3][0:rows, :], AF.Square, ["pb3"], ["pT0", "ssq2"],
                        accum=ssq[0:rows, 2:3])
                    VOP("dve", "tensor_scalar", ["ssq2"], ["rstd2"], out=rstd[0:rows, 2:3], in0=ssq[0:rows, 2:3], scalar1=1.0 / 512.0,
                        scalar2=None, op0=ALU.mult)
                    VOP("dve", "tensor_tensor", ["rstd2", "small"], ["rstd2"], out=rstd[0:rows, 2:3], in0=rstd[0:rows, 2:3],
                        in1=small[0:rows, base + EPQ + h:base + EPQ + h + 1], op=ALU.add)
                    VOP("pool", "tensor_tensor", ["rstd2", "mhalf"], ["rstd2"], out=rstd[0:rows, 2:3], in0=rstd[0:rows, 2:3],
                        in1=mhalf[0:rows, 0:1], op=ALU.pow)
                    og = pT[1 + (t % 2)]
                    ogk = f"pT{1 + (t % 2)}"
                    VOP("dve", "scalar_tensor_tensor", ["pb3", "rstd2", f"F{3 + vi}"], [ogk], out=og[0:rows, :], in0=pb[3][0:rows, :],
                        scalar=rstd[0:rows, 2:3], in1=sgl[0:rows, :], op0=ALU.mult, op1=ALU.mult)

                    def st2b(t=t, rows=rows, og=og, ogk=ogk, prompt=prompt, h=h, s=s):
                        for j in range(4):
                            TR(pt[0:128, 512 + j * 128:512 + j * 128 + rows], og[0:rows, j * 128:(j + 1) * 128], rows, [ogk, "ident"], ["pt"])
                        oi = rot("oi", 2)
                        ob = bfw[2 + oi]
                        okey = f"bfw{2 + oi}"
                        obv = ob[:, :].rearrange("p (a n) -> p a n", a=4)
                        ACT(obv[:, :, 0:rows], pt[:, 512:1024].rearrange("p (a n) -> p a n", a=4)[:, :, 0:rows], AF.Identity, ["pt"], [okey])
                        DMA("act", scr1[s, t, :, h * 4:(h + 1) * 4, 0:rows], obv[:, :, 0:rows], [okey], [f"s1_{t}"])
                    if pend2[0] is not None:
                        pend2[0]()
                    pend2[0] = st2b
                if h + 1 < 8:
                    nxt[1] = load_w(wb[h + 1, 1])
                    nxt[2] = load_w(wb[h + 1, 2])
            if pend2[0] is not None:
                pend2[0]()
                pend2[0] = None
            TRANS0(XB_KEYS)

        for s in range(2):
            ntiles = 17 if s == 0 else 16
            phase_norm(lambda t, s=s: xp[s, t * 128:(t + 1) * 128, :] if t < 16 else xs[0:64, :], gA, "gA", ntiles)
            phase_attn(s)
            phase_outproj_a(s, ntiles)
            phase_ret(s)
            phase_outproj_b(s, ntiles)

        S.assign(csems, dsems)
        _NC_CACHE["sched"] = S
        with nc.Block() as block:
            @block.tensor
            def _(e):
                S.emit("pe", e)

            @block.scalar
            def _(e):
                S.emit("act", e)

            @block.vector
            def _(e):
                S.emit("dve", e)

            @block.gpsimd
            def _(e):
                S.emit("pool", e)

            @block.sync
            def _(e):
                S.emit("sp", e, final=True)
    return nc


_NC_CACHE = {}


def _prep_shared(inp):
    f32 = np.float32
    w_in_a = np.asarray(inp["w_in_a"], f32)[0]
    wa = np.empty((16, 128, 16, 512), f32)
    for h in range(16):
        cols = np.concatenate([np.arange(h * 128, (h + 1) * 128) + off for off in (0, 2048, 4096, 6144)])
        wa[h] = w_in_a[:, cols].reshape(16, 128, 512).transpose(1, 0, 2)
    woa = np.ascontiguousarray(np.asarray(inp["w_out_a"], f32)[0].reshape(16, 128, 2048).transpose(1, 0, 2))
    w_in_b = np.asarray(inp["w_in_b"], f32)[0]
    wb = np.empty((8, 3, 128, 16, 512), f32)
    for h in range(8):
        qk = np.concatenate([np.arange(h * 256, (h + 1) * 256), 2048 + np.arange(h * 256, (h + 1) * 256)])
        blocks = [qk, 4096 + np.arange(h * 512, (h + 1) * 512), 8192 + np.arange(h * 512, (h + 1) * 512)]
        for j, cols in enumerate(blocks):
            wb[h, j] = w_in_b[:, cols].reshape(16, 128, 512).transpose(1, 0, 2)
    w_out_b = np.asarray(inp["w_out_b"], f32)[0]
    wob = np.ascontiguousarray(w_out_b.reshape(2, 16, 128, 2048).transpose(0, 2, 1, 3))
    gA = np.ascontiguousarray(np.asarray(inp["norm_a"], f32)[0].reshape(16, 128).T)
    gB = np.ascontiguousarray(np.asarray(inp["norm_b"], f32)[0].reshape(16, 128).T)
    gF = np.asarray(inp["norm_final"], f32).reshape(1, 2048)
    lam4 = np.concatenate([np.asarray(inp[k], f32)[0] for k in ("lambda_q1", "lambda_k1", "lambda_q2", "lambda_k2")]).reshape(1, 256)
    subln = np.asarray(inp["subln_a"], f32)[0].reshape(128, 1)
    shared = dict(wa=wa, woa=woa, wb=wb, wob=wob, gA=gA, gB=gB, gF=gF, lam4=lam4, subln=subln)
    shared.update(_TAB)
    return shared


def kernel(**inp):
    f32 = np.float32
    if "nc" not in _NC_CACHE:
        _NC_CACHE["nc"] = build_program()
    nc = _NC_CACHE["nc"]
    shared = _prep_shared(inp)
    x_prompt = np.asarray(inp["x_prompt"], f32)
    x_sample = np.asarray(inp["x_sample"], f32)
    cache_k = np.asarray(inp["cache_k_a"], f32)[0].reshape(16, 1024, 2048)
    cache_v = np.asarray(inp["cache_v_a"], f32)[0].reshape(16, 1024, 2048)
    state = np.asarray(inp["state_ret"], f32)[0]
    in_maps = []
    for c in range(NCORES):
        m = dict(shared)
        m["xp"] = np.ascontiguousarray(x_prompt[2 * c:2 * c + 2])
        xs = np.zeros((64, 2048), f32)
        xs[0:16] = x_sample[2 * c]
        xs[32:48] = x_sample[2 * c + 1]
        m["xs"] = xs
        m["ck"] = np.ascontiguousarray(cache_k[2 * c:2 * c + 2])
        m["cv"] = np.ascontiguousarray(cache_v[2 * c:2 * c + 2])
        m["st"] = np.ascontiguousarray(state[2 * c:2 * c + 2])
        in_maps.append(m)
    res = run_bass_kernel_spmd(nc, in_maps, core_ids=list(range(NCORES)))
    R = res.results

    def cat(name):
        return np.concatenate([np.asarray(r[name]) for r in R], axis=0)

    def cat_s(name):
        return np.concatenate([np.stack([np.asarray(r[name])[0:16], np.asarray(r[name])[32:48]]) for r in R], axis=0)

    y_prompt = cat("y_p").astype(f32)
    y_sample = cat_s("y_s").astype(f32)
    nk_p = cat("nk_p").reshape(1, 16, 2048, 32, 64).astype(f32)
    nv_p = cat("nv_p").reshape(1, 16, 2048, 16, 128).astype(f32)
    ns_p = cat("ns_p").reshape(1, 16, 8, 256, 512).astype(f32)
    nk_s = cat_s("nk_s").reshape(1, 16, 16, 32, 64).astype(f32)
    nv_s = cat_s("nv_s").reshape(1, 16, 16, 16, 128).astype(f32)
    ns_s = cat("ns_s").reshape(1, 16, 8, 256, 512).astype(f32)
    return (y_prompt, y_sample, nk_p, nv_p, ns_p, nk_s, nv_s, ns_s)
```
